# Optimizing a Trainium2 kernel written in Bass

```python
import jax, jax.numpy as jnp
from jax import lax
import numpy as np

D_MODEL = 1024
BATCH = 2
SEQ = 16384
DEPTH = 1
DEC_BATCH = 16
DEC_SEQ = 4096
PAST_LEN = 128

MIX_WIDTH = D_MODEL
POOL_WIDTH = MIX_WIDTH // 2
CONV_WIDTH = MIX_WIDTH - POOL_WIDTH
POOL_WINDOWS = (2, 4, 8, 16)
N_POOL_GROUPS = len(POOL_WINDOWS)
POOL_GROUP_DIM = POOL_WIDTH // N_POOL_GROUPS
CONV_KERNEL = 31
D_FF = 2816
IN_COLS = POOL_WIDTH + 2 * CONV_WIDTH
EPS = 1e-6

kernel_name = "hybrid_pool_conv_macaron_encoder"


def rmsnorm(x, g):
    xf = x.astype(jnp.float32)
    y = xf * lax.rsqrt(jnp.mean(xf * xf, axis=-1, keepdims=True) + EPS)
    return (y * g.astype(jnp.float32)).astype(x.dtype)


def layernorm(x, g, b):
    xf = x.astype(jnp.float32)
    mu = jnp.mean(xf, axis=-1, keepdims=True)
    var = jnp.mean(jnp.square(xf - mu), axis=-1, keepdims=True)
    y = (xf - mu) * lax.rsqrt(var + EPS)
    return (y * g.astype(jnp.float32) + b.astype(jnp.float32)).astype(x.dtype)


def swiglu_ffn(h, w_gu, w_down):
    gu = h @ w_gu
    g, u = jnp.split(gu, 2, axis=-1)
    return (jax.nn.silu(g) * u) @ w_down


def centred_mean_minus_self(xg, window):
    L = xg.shape[1]
    half = window // 2
    xf = xg.astype(jnp.float32)
    cs = jnp.concatenate([jnp.zeros_like(xf[:, :1]), jnp.cumsum(xf, axis=1)], axis=1)
    t = np.arange(L)
    lo = np.clip(t - half, 0, L)
    hi = np.clip(t + half, 0, L)
    cnt = jnp.asarray((hi - lo).astype(np.float32))[None, :, None]
    win_sum = jnp.take(cs, jnp.asarray(hi), axis=1) - jnp.take(cs, jnp.asarray(lo), axis=1)
    return (win_sum / cnt - xf).astype(xg.dtype)


def pool_mixer(u_pool, pool_w, pool_b, pool_scale):
    B, L, _ = u_pool.shape
    groups = jnp.split(u_pool, N_POOL_GROUPS, axis=-1)
    pooled = jnp.stack([centred_mean_minus_self(g, w) for g, w in zip(groups, POOL_WINDOWS)], axis=2)
    mixed = jnp.einsum("blgc,gcd->blgd", pooled, pool_w) + pool_b
    return mixed.reshape(B, L, POOL_WIDTH) * pool_scale


def conv_mixer(u_val, u_gate, dw_w, dw_b, conv_ln_g, conv_ln_b):
    v = u_val * jax.nn.sigmoid(u_gate)
    y = lax.conv_general_dilated(
        v, dw_w[:, None, :],
        window_strides=(1,),
        padding=[(CONV_KERNEL // 2, CONV_KERNEL // 2)],
        dimension_numbers=("NWC", "WIO", "NWC"),
        feature_group_count=CONV_WIDTH,
    ) + dw_b
    return jax.nn.silu(layernorm(y, conv_ln_g, conv_ln_b))


def trunk(x, ffn1_norm, ffn1_w_gu, ffn1_w_down, mix_norm, w_in, pool_w, pool_b, pool_scale,
          dw_w, dw_b, conv_ln_g, conv_ln_b, w_out, ffn2_norm, ffn2_w_gu, ffn2_w_down, final_norm):
    for l in range(DEPTH):
        x = x + 0.5 * swiglu_ffn(rmsnorm(x, ffn1_norm[l]), ffn1_w_gu[l], ffn1_w_down[l])
        u = rmsnorm(x, mix_norm[l]) @ w_in[l]
        u_pool = u[..., :POOL_WIDTH]
        u_val = u[..., POOL_WIDTH:POOL_WIDTH + CONV_WIDTH]
        u_gate = u[..., POOL_WIDTH + CONV_WIDTH:]
        a = pool_mixer(u_pool, pool_w[l], pool_b[l], pool_scale[l])
        b = conv_mixer(u_val, u_gate, dw_w[l], dw_b[l], conv_ln_g[l], conv_ln_b[l])
        x = x + jnp.concatenate([a, b], axis=-1) @ w_out[l]
        x = x + 0.5 * swiglu_ffn(rmsnorm(x, ffn2_norm[l]), ffn2_w_gu[l], ffn2_w_down[l])
    return rmsnorm(x, final_norm)


def setup_inputs(seed: int = 0) -> dict:
    key = jax.random.key(seed)
    ks = jax.random.split(key, 20)
    f32 = jnp.float32

    def nrm(k, shape, scale):
        return jax.random.normal(k, shape, f32) * scale

    return {
        "x_prompt": nrm(ks[0], (BATCH, SEQ, D_MODEL), 1.0),
        "x_sample": nrm(ks[1], (DEC_BATCH, DEC_SEQ, D_MODEL), 1.0),
        "ffn1_norm": 1.0 + nrm(ks[2], (DEPTH, D_MODEL), 0.02),
        "ffn1_w_gu": nrm(ks[3], (DEPTH, D_MODEL, 2 * D_FF), D_MODEL ** -0.5),
        "ffn1_w_down": nrm(ks[4], (DEPTH, D_FF, D_MODEL), D_FF ** -0.5),
        "mix_norm": 1.0 + nrm(ks[5], (DEPTH, D_MODEL), 0.02),
        "w_in": nrm(ks[6], (DEPTH, D_MODEL, IN_COLS), D_MODEL ** -0.5),
        "pool_w": nrm(ks[7], (DEPTH, N_POOL_GROUPS, POOL_GROUP_DIM, POOL_GROUP_DIM), POOL_GROUP_DIM ** -0.5),
        "pool_b": nrm(ks[8], (DEPTH, N_POOL_GROUPS, POOL_GROUP_DIM), 0.02),
        "pool_scale": 1.0 + nrm(ks[9], (DEPTH, POOL_WIDTH), 0.1),
        "dw_w": nrm(ks[10], (DEPTH, CONV_KERNEL, CONV_WIDTH), CONV_KERNEL ** -0.5),
        "dw_b": nrm(ks[11], (DEPTH, CONV_WIDTH), 0.02),
        "conv_ln_g": 1.0 + nrm(ks[12], (DEPTH, CONV_WIDTH), 0.02),
        "conv_ln_b": nrm(ks[13], (DEPTH, CONV_WIDTH), 0.02),
        "w_out": nrm(ks[14], (DEPTH, MIX_WIDTH, D_MODEL), MIX_WIDTH ** -0.5),
        "ffn2_norm": 1.0 + nrm(ks[15], (DEPTH, D_MODEL), 0.02),
        "ffn2_w_gu": nrm(ks[16], (DEPTH, D_MODEL, 2 * D_FF), D_MODEL ** -0.5),
        "ffn2_w_down": nrm(ks[17], (DEPTH, D_FF, D_MODEL), D_FF ** -0.5),
        "final_norm": 1.0 + nrm(ks[18], (D_MODEL,), 0.02),
    }


def reference(x_prompt, x_sample, ffn1_norm, ffn1_w_gu, ffn1_w_down, mix_norm, w_in, pool_w, pool_b,
              pool_scale, dw_w, dw_b, conv_ln_g, conv_ln_b, w_out, ffn2_norm, ffn2_w_gu, ffn2_w_down,
              final_norm):
    y_prompt = trunk(x_prompt, ffn1_norm, ffn1_w_gu, ffn1_w_down, mix_norm, w_in, pool_w, pool_b,
                     pool_scale, dw_w, dw_b, conv_ln_g, conv_ln_b, w_out, ffn2_norm, ffn2_w_gu,
                     ffn2_w_down, final_norm)
    y_sample = trunk(x_sample, ffn1_norm, ffn1_w_gu, ffn1_w_down, mix_norm, w_in, pool_w, pool_b,
                     pool_scale, dw_w, dw_b, conv_ln_g, conv_ln_b, w_out, ffn2_norm, ffn2_w_gu,
                     ffn2_w_down, final_norm)
    return (y_prompt, y_sample)
```

```python
import numpy as np
import concourse.bass as bass
import concourse.mybir as mybir
from concourse.bass_utils import run_bass_kernel_spmd

F32 = mybir.dt.float32
BF16 = mybir.dt.bfloat16
AF = mybir.ActivationFunctionType
ALU = mybir.AluOpType

D = 1024
DFF = 2816
NFC = 22
TT = 1024
HALO = 16
NCOL = TT + 2 * HALO
NCH = 9
NXS = 10
NRING = 8
EPS = 1e-6
GROUPS = [(0, 8), (8, 15), (15, 22)]
PIECES_A = [(0, 512), (512, 1024), (1024, 1056)]
PIECES_B = [(0, 512), (512, 1024)]
NSMALL = 168
ENGS = ["pe", "act", "dve", "pool", "sp"]
NDSEM = {"sp": 12, "pool": 12, "act": 10}


def blocks_of(lo, hi):
    return range(lo // 128, (hi - 1) // 128 + 1)


class Op:
    __slots__ = ("eng", "fn", "deps", "ddeps", "signal", "idx", "dma", "clock")


class Prog:
    def __init__(self):
        self.eng_ops = {e: [] for e in ENGS}
        self.last_writer = {}
        self.readers = {}
        self.known = {e: {} for e in ENGS}
        self.dma_count = {q: [0] * n for q, n in NDSEM.items()}
        self.dma_rr = {q: 0 for q in NDSEM}

    def _collect(self, eng, reads, writes):
        toks = []
        for k in reads:
            w = self.last_writer.get(k)
            if w is not None:
                toks.append(w)
        for k in writes:
            w = self.last_writer.get(k)
            if w is not None:
                toks.append(w)
            toks.extend(self.readers.get(k, ()))
        need = {}
        for t in toks:
            if t[0] == "e" and t[1] == "pe" and eng == "pe":
                continue
            key = (t[0], t[1])
            if need.get(key, -1) < t[2]:
                need[key] = t[2]
        kn = self.known[eng]
        out = {}
        for key, v in need.items():
            if kn.get(key, -1) < v:
                out[key] = v
        return out

    def _learn(self, eng, need):
        kn = self.known[eng]
        for key, v in need.items():
            if kn.get(key, -1) < v:
                kn[key] = v
            if key[0] == "e":
                clk = self.eng_ops[key[1]][v].clock
                for k2, v2 in clk.items():
                    if kn.get(k2, -1) < v2:
                        kn[k2] = v2

    def op(self, eng, fn, reads=(), writes=(), dma=False):
        need = self._collect(eng, reads, writes)
        o = Op()
        o.eng = eng
        o.fn = fn
        o.deps = need
        o.signal = False
        o.dma = None
        o.idx = len(self.eng_ops[eng])
        for key, v in need.items():
            if key[0] == "e":
                self.eng_ops[key[1]][v].signal = True
        self._learn(eng, need)
        o.clock = dict(self.known[eng])
        self.eng_ops[eng].append(o)
        if dma:
            q = eng
            si = self.dma_rr[q]
            self.dma_rr[q] = (si + 1) % NDSEM[q]
            self.dma_count[q][si] += 1
            tok = ("d", (q, si), self.dma_count[q][si] * 16)
            o.dma = (q, si)
        else:
            tok = ("e", eng, o.idx)
        for k in writes:
            self.last_writer[k] = tok
            self.readers[k] = []
        for k in reads:
            self.readers.setdefault(k, []).append(tok)
        return o

    def emit(self, nc, block, esem, dsem, final_waits):
        for e in ENGS:
            ms = 0
            for o in self.eng_ops[e]:
                if o.signal:
                    ms += 1
                    o.signal = ms
        handles = {"pe": block.tensor, "act": block.scalar, "dve": block.vector,
                   "pool": block.gpsimd, "sp": block.sync}
        prog = self

        def make(ename):
            def body(eng):
                for o in prog.eng_ops[ename]:
                    for key, v in o.deps.items():
                        if key[0] == "e":
                            eng.wait_ge(esem[key[1]], prog.eng_ops[key[1]][v].signal)
                        else:
                            eng.wait_ge(dsem[key[1]], v)
                    ins = o.fn(eng)
                    if o.dma is not None:
                        ins.then_inc(dsem[o.dma], 16)
                    elif o.signal:
                        ins.then_inc(esem[ename], 1)
                if ename == "sp":
                    for key, v in final_waits:
                        eng.wait_ge(dsem[key], v)
            return body

        for e in ENGS:
            handles[e](make(e))


def build_nc(ntiles):
    nc = bass.Bass("TRN2", target_bir_lowering=False)
    P = Prog()

    def dram_in(name, shape, dt=F32):
        return nc.dram_tensor(name, list(shape), dt, kind="ExternalInput").ap()

    xt = dram_in("xt", [ntiles, NCOL, D])
    w_gu = [dram_in("w_gu1", [D, 2 * DFF]), dram_in("w_gu2", [D, 2 * DFF])]
    w_dn = [dram_in("w_dn1", [DFF, D]), dram_in("w_dn2", [DFF, D])]
    w_in = dram_in("w_in", [D, 1536])
    w_out = dram_in("w_out", [D, D])
    pool_w = dram_in("pool_w", [512, 128])
    small_d = dram_in("small", [128, NSMALL])
    gfin_d = dram_in("gfin", [1, D])
    pcorr_d = dram_in("pcorr", [1, ntiles * 64])
    ident_d = dram_in("ident", [128, 128])
    yt = nc.dram_tensor("yt", [ntiles, TT, D], F32, kind="ExternalOutput").ap()

    sc_gu = [nc.dram_tensor("sc_gu%d" % i, [NFC, 128, 2048], BF16).ap() for i in (1, 2)]
    sc_dn = [nc.dram_tensor("sc_dn%d" % i, [NFC, 128, D], BF16).ap() for i in (1, 2)]
    sc_in = nc.dram_tensor("sc_in", [6, 128, 2048], BF16).ap()
    sc_out = nc.dram_tensor("sc_out", [8, 128, D], BF16).ap()

    import contextlib
    es = contextlib.ExitStack()
    with es:
        def sb(name, shape, dt):
            return es.enter_context(nc.sbuf_tensor(name, list(shape), dt))

        def ps(name, shape, dt):
            return es.enter_context(nc.psum_tensor(name, list(shape), dt))

        xs = sb("xs", [128, NXS, D], F32)
        hn = sb("hn", [128, 2, D], BF16)
        hb = sb("hb", [128, 8, NCOL], BF16)
        b16 = sb("b16", [128, 8, NCOL], BF16)
        sg = sb("sg", [128, 2, 512], F32)
        upool = sb("upool", [128, 2, NCOL], F32)
        ptmp = sb("ptmp", [128, 2, NCOL], F32)
        ybuf = sb("ybuf", [128, 4, 512], F32)
        ysq = sb("ysq", [128, 2, 512], F32)
        lnt = sb("lnt", [128, 4, 512], F32)
        tnb = sb("tnb", [128, 2, 512], F32)
        diag = sb("diag", [128, 4, 31, 128], BF16)
        ring = sb("ring", [128, NRING, 2048], BF16)
        ostg = sb("ostg", [128, 2, D], F32)
        gfb = sb("gfb_sb", [128, D], F32)
        small = sb("small_sb", [128, NSMALL], F32)
        whalf = sb("whalf", [128, 124], F32)
        pbias = sb("pbias", [128, 4], F32)
        identf = sb("identf", [128, 128], F32)
        identb = sb("identb", [128, 128], BF16)
        onesm = sb("onesm", [128, 128], F32)
        mhalf = sb("mhalf", [128, 1], F32)
        poolw = sb("poolw", [128, 4, 128], BF16)
        pcorr = sb("pcorr_sb", [128, ntiles * 64], F32)
        ss = sb("ss", [128, 16], F32)
        rst = sb("rst", [128, 16], F32)
        sqj = sb("sqj", [128, D], BF16)

        pbank = [ps("pb%d" % i, [128, 512], F32) for i in range(6)]
        tbank = [ps("tb%d" % i, [128, 1024], BF16) for i in range(2)]

        esem = {e: es.enter_context(nc.semaphore("e_" + e)) for e in ENGS}
        dsem = {}
        for q, n in NDSEM.items():
            for i in range(n):
                dsem[(q, i)] = es.enter_context(nc.semaphore("d_%s%d" % (q, i)))

        cnt = {"up": 0, "acc": 0, "tp": 0, "ring": 0, "ssc": 0, "hn": 0, "sg": 0, "ost": 0}

        def nxt(name, mod):
            v = cnt[name]
            cnt[name] = v + 1
            return v % mod

        def dma(q, out_ap, in_ap, reads, writes):
            return P.op(q, lambda e, o=out_ap, i=in_ap: e.dma_start(out=o, in_=i),
                        reads=reads, writes=writes, dma=True)

        def mm_group(out_ap, pairs, reads, writes):
            def fn(e, out_ap=out_ap, pairs=pairs):
                n = len(pairs)
                ins = None
                for i, (l, r) in enumerate(pairs):
                    ins = e.matmul(out_ap, l, r, start=(i == 0), stop=(i == n - 1))
                return ins
            return P.op("pe", fn, reads=reads, writes=writes)

        dma("sp", small[:], small_d, [], [("small",)])
        dma("sp", identf[:], ident_d, [], [("identf",)])
        dma("sp", gfb[:], gfin_d.partition_broadcast(128), [], [("gfb",)])
        dma("sp", pcorr[:], pcorr_d.partition_broadcast(128), [], [("pcorr",)])
        P.op("dve", lambda e: e.tensor_copy(out=identb[:], in_=identf[:]), [("identf",)], [("identb",)])
        P.op("dve", lambda e: e.memset(onesm[:], 1.0 / 512.0), [], [("onesm",)])
        P.op("pool", lambda e: e.memset(mhalf[:], -0.5), [], [("mhalf",)])
        P.op("dve", lambda e: e.tensor_scalar(out=whalf[:], in0=small[:, 44:168], scalar1=0.5, scalar2=None,
                                              op0=ALU.mult), [("small",)], [("whalf",)])
        P.op("dve", lambda e: e.tensor_tensor(out=pbias[:], in0=small[:, 24:28], in1=small[:, 28:32],
                                              op=ALU.mult), [("small",)], [("pbias",)])
        for i in range(4):
            for k in range(31):
                P.op("dve", lambda e, i=i, k=k: e.tensor_scalar(
                    out=diag[:, i, k, :], in0=identf[:], scalar1=whalf[:, i * 31 + k:i * 31 + k + 1],
                    scalar2=None, op0=ALU.mult),
                    [("identf",), ("whalf",)], [("diag", i, k)])

        pro = {"st": 0}

        def stage_unit(srcs, gain_cols, dst_ap, dst_key, shape3):
            s = pro["st"] % (NXS // 2)
            r = pro["st"] % NRING
            pro["st"] += 1
            a, b = shape3
            st32 = xs[:, 2 * s:2 * s + 2, :]
            st32v = st32.rearrange("p s d -> p (s d)").rearrange("p (a b) -> p a b", a=a)
            st16v = ring[:, r, :].rearrange("p (a b) -> p a b", a=a)
            k32 = [("xs", 2 * s), ("xs", 2 * s + 1)]
            for (lo, hi), src in srcs:
                dma("sp", st32v[:, :, lo:hi], src, [], k32)
            if gain_cols is not None:
                g0 = gain_cols
                gap = small[:, g0:g0 + a].unsqueeze(2).to_broadcast([128, a, b])
                P.op("dve", lambda e, o=st16v, i=st32v, g=gap: e.tensor_tensor(out=o, in0=i, in1=g, op=ALU.mult),
                     k32 + [("small",)], [("ring", r)])
            else:
                P.op("dve", lambda e, o=st16v, i=st32v: e.tensor_copy(out=o, in_=i), k32, [("ring", r)])
            dma("act", dst_ap, st16v, [("ring", r)], [dst_key])

        for f in range(2):
            wv = w_gu[f].rearrange("(kc p) f -> p kc f", p=128)
            for j in range(NFC):
                stage_unit([((0, 128), wv[:, :, j * 128:(j + 1) * 128]),
                            ((128, 256), wv[:, :, DFF + j * 128:DFF + (j + 1) * 128])],
                           0 if f == 0 else 16,
                           sc_gu[f][j].rearrange("p (a b) -> p a b", a=8), ("sc_gu", f, j), (8, 256))
            if f == 0:
                wiv = w_in.rearrange("(kc p) f -> p kc f", p=128)
                for u in range(6):
                    if u < 2:
                        srcs = [((0, 256), wiv[:, :, u * 256:(u + 1) * 256])]
                    else:
                        i = u - 2
                        srcs = [((0, 128), wiv[:, :, 512 + i * 128:512 + (i + 1) * 128]),
                                ((128, 256), wiv[:, :, 1024 + i * 128:1024 + (i + 1) * 128])]
                    stage_unit(srcs, 8, sc_in[u].rearrange("p (a b) -> p a b", a=8), ("sc_in", u), (8, 256))
            wdv = w_dn[f].rearrange("(j p) d -> p j d", p=128)
            for q in range(NFC // 2):
                stage_unit([((0, D), wdv[:, 2 * q:2 * q + 2, :])], None,
                           sc_dn[f][2 * q:2 * q + 2].rearrange("j p d -> p j d"), ("sc_dn", f, q), (2, D))
            if f == 0:
                wov = w_out.rearrange("(j p) d -> p j d", p=128)
                for q in range(4):
                    stage_unit([((0, D), wov[:, 2 * q:2 * q + 2, :])], None,
                               sc_out[2 * q:2 * q + 2].rearrange("j p d -> p j d"), ("sc_out", q), (2, D))
        pwst = xs[:, 0, 0:512].rearrange("p (g d) -> p g d", g=4)
        dma("sp", pwst, pool_w.rearrange("(g c) d -> c g d", g=4), [], [("xs", 0)])
        P.op("dve", lambda e: e.tensor_copy(out=poolw[:], in_=pwst), [("xs", 0)], [("poolw",)])

        def load_unit(src_ap, src_key, view=None):
            r = nxt("ring", NRING)
            dst = ring[:, r, :] if view is None else view(ring[:, r, :])
            dma("sp", dst, src_ap, [src_key], [("ring", r)])
            return r

        def xslot(ti, c):
            return (ti * NCH + c) % NXS

        def rms_stats(slot, rows):
            col = nxt("ssc", 16)
            P.op("act", lambda e: e.activation(out=sqj[:rows, :], in_=xs[:rows, slot, :], func=AF.Square,
                                               accum_out=ss[:rows, col:col + 1]),
                 [("xs", slot)], [("sqj",), ("ss", col)])
            P.op("pool", lambda e: e.tensor_scalar(out=ss[:rows, col:col + 1], in0=ss[:rows, col:col + 1],
                                                   scalar1=1.0 / D, scalar2=EPS, op0=ALU.mult, op1=ALU.add),
                 [("ss", col)], [("ss", col)])
            P.op("pool", lambda e: e.tensor_tensor(out=rst[:rows, col:col + 1], in0=ss[:rows, col:col + 1],
                                                   in1=mhalf[:rows, :], op=ALU.pow),
                 [("ss", col), ("mhalf",)], [("rst", col)])
            return col

        def norm_scale(slot, rows, col):
            h = nxt("hn", 2)
            P.op("act", lambda e: e.activation(out=hn[:rows, h, :], in_=xs[:rows, slot, :], func=AF.Copy,
                                               scale=rst[:rows, col:col + 1]),
                 [("xs", slot), ("rst", col)], [("hn", h)])
            return h

        def norm_transpose(h, c):
            rows = 128 if c < 8 else 32
            t = nxt("tp", 2)
            tb = tbank[t]

            def fn(e):
                ins = None
                for kc in range(8):
                    ins = e.transpose(tb[:, kc * 128:kc * 128 + rows], hn[:rows, h, kc * 128:(kc + 1) * 128],
                                      identb[:rows, :rows])
                return ins
            P.op("pe", fn, [("hn", h), ("identb",)], [("tb", t)])
            src = tb[:, :].rearrange("p (k t) -> p k t", k=8)[:, :, 0:rows]
            lo = c * 128
            P.op("act", lambda e: e.activation(out=hb[:, :, lo:lo + rows], in_=src, func=AF.Copy),
                 [("tb", t)], [("hb", kc, c) for kc in range(8)])

        def ffn(ti, f, nchunks, pieces):
            for (j0, j1) in GROUPS:
                for j in range(j0, j1):
                    r = load_unit(sc_gu[f][j], ("sc_gu", f, j))
                    wu = ring[:, r, :].rearrange("p (k c) -> p k c", k=8)
                    jj = j - j0
                    for (lo, hi) in pieces:
                        n = hi - lo
                        u = nxt("up", 2)
                        pg, pu = pbank[2 * u], pbank[2 * u + 1]
                        hkeys = [("hb", kc, b) for kc in range(8) for b in blocks_of(lo, hi)]
                        mm_group(pg[:, :n], [(wu[:, kc, 0:128], hb[:, kc, lo:hi]) for kc in range(8)],
                                 hkeys + [("ring", r)], [("pb", 2 * u)])
                        mm_group(pu[:, :n], [(wu[:, kc, 128:256], hb[:, kc, lo:hi]) for kc in range(8)],
                                 hkeys + [("ring", r)], [("pb", 2 * u + 1)])
                        s = nxt("sg", 2)
                        P.op("act", lambda e, pg=pg, n=n, s=s: e.activation(out=sg[:, s, :n], in_=pg[:, :n], func=AF.Silu),
                             [("pb", 2 * u)], [("sg", s)])
                        P.op("dve", lambda e, pu=pu, n=n, s=s, jj=jj, lo=lo, hi=hi: e.tensor_tensor(
                            out=b16[:, jj, lo:hi], in0=sg[:, s, :n], in1=pu[:, :n], op=ALU.mult),
                            [("sg", s), ("pb", 2 * u + 1)], [("b16", jj, b) for b in blocks_of(lo, hi)])
                nj = j1 - j0
                slots = []
                jq = j0
                while jq < j1:
                    if jq % 2 == 0 and jq + 1 < j1:
                        r = load_unit(sc_dn[f][jq:jq + 2].rearrange("j p d -> p j d"), ("sc_dn", f, jq // 2),
                                      view=lambda a: a.rearrange("p (j d) -> p j d", j=2))
                        slots.append((r, 0))
                        slots.append((r, 1))
                        jq += 2
                    else:
                        r = load_unit(sc_dn[f][jq], ("sc_dn", f, jq // 2),
                                      view=lambda a: a[:, 0:D])
                        slots.append((r, 0))
                        jq += 1
                for c in range(nchunks):
                    rows = 128 if c < 8 else 32
                    lo = c * 128
                    slot = xslot(ti, c)
                    for half in range(2):
                        a = 4 + nxt("acc", 2)
                        pa = pbank[a]
                        pairs = []
                        for jj in range(nj):
                            r, o = slots[jj]
                            pairs.append((b16[:, jj, lo:lo + rows],
                                          ring[:, r, o * D + half * 512:o * D + (half + 1) * 512]))
                        mm_group(pa[:rows, :], pairs,
                                 [("b16", jj, c) for jj in range(nj)] + [("ring", r) for r, _ in slots],
                                 [("pb", a)])
                        P.op("dve", lambda e, pa=pa, rows=rows, slot=slot, half=half: e.scalar_tensor_tensor(
                            out=xs[:rows, slot, half * 512:(half + 1) * 512], in0=pa[:rows, :], scalar=0.5,
                            in1=xs[:rows, slot, half * 512:(half + 1) * 512], op0=ALU.mult, op1=ALU.add),
                            [("pb", a), ("xs", slot)], [("xs", slot)])

        def norm_all(ti, nchunks):
            cols = {}
            for c in range(nchunks):
                rows = 128 if c < 8 else 32
                cols[c] = rms_stats(xslot(ti, c), rows)
            for c in range(nchunks):
                rows = 128 if c < 8 else 32
                h = norm_scale(xslot(ti, c), rows, cols[c])
                norm_transpose(h, c)

        def tcols(lo, hi):
            if lo < 1024:
                return [(0, hi - lo, 16 + lo)]
            return [(0, 16, 0), (16, 32, 1040)]

        def mixer(ti):
            for u in range(2):
                r = load_unit(sc_in[u], ("sc_in", u))
                wu = ring[:, r, :].rearrange("p (k c) -> p k c", k=8)
                for gg in range(2):
                    g = 2 * u + gg
                    ur = g % 2
                    for (lo, hi) in PIECES_A:
                        n = hi - lo
                        a = 4 + nxt("acc", 2)
                        pa = pbank[a]
                        hkeys = [("hb", kc, b) for kc in range(8) for b in blocks_of(lo, hi)]
                        mm_group(pa[:, :n], [(wu[:, kc, gg * 128:(gg + 1) * 128], hb[:, kc, lo:hi]) for kc in range(8)],
                                 hkeys + [("ring", r)], [("pb", a)])
                        for (p0, p1, t0) in tcols(lo, hi):
                            P.op("act", lambda e, pa=pa, p0=p0, p1=p1, t0=t0, ur=ur: e.activation(
                                out=upool[:, ur, t0:t0 + (p1 - p0)], in_=pa[:, p0:p1], func=AF.Copy),
                                [("pb", a)], [("upool", ur, b) for b in blocks_of(t0, t0 + p1 - p0)])
                    U = upool[:, ur, :]
                    ukeys = [("upool", ur, b) for b in range(9)]
                    w = 2 ** (g + 1)
                    hlf = w // 2
                    cur = U
                    curkeys = ukeys
                    ext = 1056
                    step = 1
                    for lvl in range(g + 1):
                        ext2 = ext - step
                        dst = ptmp[:, lvl % 2, :]
                        P.op("pool", lambda e, cur=cur, dst=dst, ext2=ext2, step=step: e.tensor_tensor(
                            out=dst[:, 0:ext2], in0=cur[:, 0:ext2], in1=cur[:, step:step + ext2], op=ALU.add),
                            curkeys, [("ptmp", lvl % 2)])
                        cur = dst
                        curkeys = [("ptmp", lvl % 2)]
                        ext = ext2
                        step *= 2
                    sidx = g % 2
                    other = ptmp[:, (g + 1) % 2, :]
                    P.op("pool", lambda e, cur=cur, other=other, hlf=hlf, w=w: e.tensor_scalar(
                        out=other[:, 0:TT], in0=cur[:, 16 - hlf:16 - hlf + TT], scalar1=1.0 / w, scalar2=None,
                        op0=ALU.mult), curkeys, [("ptmp", (g + 1) % 2)])
                    okeys = [("ptmp", (g + 1) % 2)]
                    cb = ti * 64 + g * 16
                    P.op("pool", lambda e, other=other, cb=cb: e.tensor_tensor(
                        out=other[:, 0:8], in0=other[:, 0:8], in1=pcorr[:, cb:cb + 8], op=ALU.mult),
                        okeys + [("pcorr",)], okeys)
                    P.op("pool", lambda e, other=other, cb=cb: e.tensor_tensor(
                        out=other[:, TT - 8:TT], in0=other[:, TT - 8:TT], in1=pcorr[:, cb + 8:cb + 16], op=ALU.mult),
                        okeys + [("pcorr",)], okeys)
                    prow = 4 + g
                    P.op("pool", lambda e, other=other, U=U, prow=prow: e.tensor_tensor(
                        out=b16[:, prow, 0:TT], in0=other[:, 0:TT], in1=U[:, 16:16 + TT], op=ALU.subtract),
                        okeys + ukeys, [("b16", prow, b) for b in range(8)])
            for i in range(4):
                r = load_unit(sc_in[2 + i], ("sc_in", 2 + i))
                wu = ring[:, r, :].rearrange("p (k c) -> p k c", k=8)
                for (lo, hi) in PIECES_A:
                    n = hi - lo
                    u = nxt("up", 2)
                    pv, pgt = pbank[2 * u], pbank[2 * u + 1]
                    hkeys = [("hb", kc, b) for kc in range(8) for b in blocks_of(lo, hi)]
                    mm_group(pv[:, :n], [(wu[:, kc, 0:128], hb[:, kc, lo:hi]) for kc in range(8)],
                             hkeys + [("ring", r)], [("pb", 2 * u)])
                    mm_group(pgt[:, :n], [(wu[:, kc, 128:256], hb[:, kc, lo:hi]) for kc in range(8)],
                             hkeys + [("ring", r)], [("pb", 2 * u + 1)])
                    s = nxt("sg", 2)
                    P.op("act", lambda e, pgt=pgt, n=n, s=s: e.activation(out=sg[:, s, :n], in_=pgt[:, :n],
                                                                          func=AF.Tanh, scale=0.5),
                         [("pb", 2 * u + 1)], [("sg", s)])
                    for (p0, p1, t0) in tcols(lo, hi):
                        P.op("dve", lambda e, pv=pv, s=s, p0=p0, p1=p1, t0=t0, i=i: e.scalar_tensor_tensor(
                            out=b16[:, i, t0:t0 + (p1 - p0)], in0=sg[:, s, p0:p1], scalar=1.0, in1=pv[:, p0:p1],
                            op0=ALU.add, op1=ALU.mult),
                            [("sg", s), ("pb", 2 * u)], [("b16", i, b) for b in blocks_of(t0, t0 + p1 - p0)])

        pend_pool = []

        def mixer_tail(ti):
            for g in range(4):
              for (lo, hi) in PIECES_B:
                a = 4 + nxt("acc", 2)
                pa = pbank[a]
                prow = 4 + g
                mm_group(pa[:, :], [(poolw[:, g, :], b16[:, prow, lo:hi])],
                         [("b16", prow, b) for b in blocks_of(lo, hi)] + [("poolw",)], [("pb", a)])
                P.op("act", lambda e, pa=pa, g=g, lo=lo, hi=hi: e.activation(
                    out=hb[:, g, lo:hi], in_=pa[:, :], func=AF.Identity,
                    scale=small[:, 28 + g:29 + g], bias=pbias[:, g:g + 1]),
                    [("pb", a), ("small",), ("pbias",)], [("hb", g, b) for b in blocks_of(lo, hi)])
            for (lo, hi) in PIECES_B:
                pm, pe2 = pbank[0], pbank[1]
                for i in range(4):
                    a = 4 + nxt("acc", 2)
                    pa = pbank[a]
                    pairs = [(diag[:, i, k, :], b16[:, i, lo + k + 1:lo + k + 1 + 512]) for k in range(31)]
                    mm_group(pa[:, :], pairs,
                             [("b16", i, b) for b in blocks_of(lo + 1, lo + 31 + 512)] +
                             [("diag", i, k) for k in range(31)], [("pb", a)])
                    P.op("act", lambda e, pa=pa, i=i: e.activation(out=ybuf[:, i, :], in_=pa[:, :], func=AF.Identity,
                                                                   bias=small[:, 32 + i:33 + i]),
                         [("pb", a), ("small",)], [("ybuf", i)])
                    q = i % 2
                    P.op("act", lambda e, pa=pa, i=i, q=q: e.activation(out=ysq[:, q, :], in_=pa[:, :], func=AF.Square,
                                                                        bias=small[:, 32 + i:33 + i]),
                         [("pb", a), ("small",)], [("ysq", q)])
                    P.op("pe", lambda e, i=i: e.matmul(pm[:, :], onesm[:, :], ybuf[:, i, :], start=(i == 0), stop=(i == 3)),
                         [("ybuf", i), ("onesm",)], [("pb", 0)])
                    P.op("pe", lambda e, i=i, q=q: e.matmul(pe2[:, :], onesm[:, :], ysq[:, q, :], start=(i == 0), stop=(i == 3)),
                         [("ysq", q), ("onesm",)], [("pb", 1)])
                P.op("act", lambda e: e.activation(out=lnt[:, 0, :], in_=pm[:, :], func=AF.Copy), [("pb", 0)], [("lnt", 0)])
                P.op("act", lambda e: e.activation(out=lnt[:, 1, :], in_=pm[:, :], func=AF.Square), [("pb", 0)], [("lnt", 1)])
                P.op("dve", lambda e: e.scalar_tensor_tensor(out=lnt[:, 2, :], in0=pe2[:, :], scalar=EPS, in1=lnt[:, 1, :],
                                                             op0=ALU.add, op1=ALU.subtract),
                     [("pb", 1), ("lnt", 1)], [("lnt", 2)])
                P.op("pool", lambda e: e.tensor_tensor(out=lnt[:, 3, :], in0=lnt[:, 2, :],
                                                       in1=mhalf[:, 0:1].to_broadcast([128, 512]), op=ALU.pow),
                     [("lnt", 2), ("mhalf",)], [("lnt", 3)])
                for i in range(4):
                    q = i % 2
                    P.op("dve", lambda e, i=i, q=q: e.tensor_tensor(out=tnb[:, q, :], in0=ybuf[:, i, :], in1=lnt[:, 0, :],
                                                                    op=ALU.subtract),
                         [("ybuf", i), ("lnt", 0)], [("tnb", q)])
                    P.op("dve", lambda e, q=q: e.tensor_tensor(out=tnb[:, q, :], in0=tnb[:, q, :], in1=lnt[:, 3, :],
                                                               op=ALU.mult),
                         [("tnb", q), ("lnt", 3)], [("tnb", q)])
                    P.op("act", lambda e, i=i, q=q, lo=lo, hi=hi: e.activation(
                        out=hb[:, 4 + i, lo:hi], in_=tnb[:, q, :], func=AF.Silu,
                        scale=small[:, 36 + i:37 + i], bias=small[:, 40 + i:41 + i]),
                        [("tnb", q), ("small",)], [("hb", 4 + i, b) for b in blocks_of(lo, hi)])
            slots = []
            for q in range(4):
                r = load_unit(sc_out[2 * q:2 * q + 2].rearrange("j p d -> p j d"), ("sc_out", q),
                              view=lambda a: a.rearrange("p (j d) -> p j d", j=2))
                slots += [(r, 0), (r, 1)]
            for c in range(8):
                lo = c * 128
                slot = xslot(ti, c)
                for half in range(2):
                    a = 4 + nxt("acc", 2)
                    pa = pbank[a]
                    pairs = [(hb[:, ci, lo:lo + 128],
                              ring[:, slots[ci][0], slots[ci][1] * D + half * 512:slots[ci][1] * D + (half + 1) * 512])
                             for ci in range(8)]
                    mm_group(pa[:, :], pairs, [("hb", ci, c) for ci in range(8)] + [("ring", r) for r, _ in slots],
                             [("pb", a)])
                    P.op("dve", lambda e, pa=pa, slot=slot, half=half: e.tensor_tensor(
                        out=xs[:, slot, half * 512:(half + 1) * 512], in0=pa[:, :],
                        in1=xs[:, slot, half * 512:(half + 1) * 512], op=ALU.add),
                        [("pb", a), ("xs", slot)], [("xs", slot)])

        def load_x(ti):
            for c in range(NCH):
                slot = xslot(ti, c)
                if c < 8:
                    dma("pool", xs[:, slot, :], xt[ti, c * 128:(c + 1) * 128, :], [], [("xs", slot)])
                else:
                    dma("pool", xs[:32, slot, :], xt[ti, TT:TT + 32, :], [], [("xs", slot)])

        load_x(0)
        out_tokens = []
        for ti in range(ntiles):
            norm_all(ti, NCH)
            ffn(ti, 0, NCH, PIECES_A)
            norm_all(ti, NCH)
            mixer(ti)
            mixer_tail(ti)
            norm_all(ti, 8)
            ffn(ti, 1, 8, PIECES_B)
            for c in range(8):
                slot = xslot(ti, c)
                col = rms_stats(slot, 128)
                o = nxt("ost", 2)
                P.op("dve", lambda e, slot=slot, col=col, o=o: e.scalar_tensor_tensor(
                    out=ostg[:, o, :], in0=xs[:, slot, :], scalar=rst[:, col:col + 1], in1=gfb[:, :],
                    op0=ALU.mult, op1=ALU.mult),
                    [("xs", slot), ("rst", col), ("gfb",)], [("ostg", o)])
                od = dma("pool", yt[ti, c * 128:(c + 1) * 128, :], ostg[:, o, :], [("ostg", o)], [("yt", ti, c)])
                out_tokens.append((od.dma, P.dma_count[od.dma[0]][od.dma[1]] * 16))
            if ti + 1 < ntiles:
                load_x(ti + 1)

        with nc.Block() as block:
            fw = [(("pool", i), P.dma_count["pool"][i] * 16) for i in range(NDSEM["pool"]) if P.dma_count["pool"][i] > 0]
            P.emit(nc, block, esem, dsem, fw)
    return nc


def _core_inputs(x_prompt, x_sample, core, ntiles=12):
    segs = []
    for s in range(2):
        segs.append((x_sample[2 * core + s], 0, 4096))
    b, q = core // 4, core % 4
    segs.append((x_prompt[b], q * 4096, (q + 1) * 4096))
    xt = np.zeros((ntiles, NCOL, D), np.float32)
    pc = np.ones((ntiles, 4, 16), np.float32)
    ti = 0
    for (seq, s0, s1) in segs:
        L = seq.shape[0]
        for i in range((s1 - s0) // TT):
            if ti >= ntiles:
                break
            t0 = s0 + i * TT
            xt[ti, :TT] = seq[t0:t0 + TT]
            if t0 - HALO >= 0:
                xt[ti, TT:TT + HALO] = seq[t0 - HALO:t0]
            if t0 + TT + HALO <= L:
                xt[ti, TT + HALO:] = seq[t0 + TT:t0 + TT + HALO]
            for g in range(4):
                w = 2 ** (g + 1)
                h = w // 2
                for m in range(8):
                    t = t0 + m
                    c = min(t + h, L) - max(t - h, 0)
                    pc[ti, g, m] = w / c
                    t = t0 + TT - 8 + m
                    c = min(t + h, L) - max(t - h, 0)
                    pc[ti, g, 8 + m] = w / c
            ti += 1
    return xt, pc.reshape(1, -1)


def _small(inp):
    sm = np.zeros((128, NSMALL), np.float32)
    sm[:, 0:8] = inp["ffn1_norm"][0].reshape(8, 128).T
    sm[:, 8:16] = inp["mix_norm"][0].reshape(8, 128).T
    sm[:, 16:24] = inp["ffn2_norm"][0].reshape(8, 128).T
    sm[:, 24:28] = inp["pool_b"][0].T
    sm[:, 28:32] = inp["pool_scale"][0].reshape(4, 128).T
    sm[:, 32:36] = inp["dw_b"][0].reshape(4, 128).T
    sm[:, 36:40] = inp["conv_ln_g"][0].reshape(4, 128).T
    sm[:, 40:44] = inp["conv_ln_b"][0].reshape(4, 128).T
    dw = inp["dw_w"][0]
    sm[:, 44:168] = dw.reshape(31, 4, 128).transpose(2, 1, 0).reshape(128, 124)
    return sm


_NC_CACHE = {}


def _shared_maps(inp):
    f = lambda a: np.ascontiguousarray(np.asarray(a, dtype=np.float32))
    return {
        "w_gu1": f(inp["ffn1_w_gu"][0]), "w_gu2": f(inp["ffn2_w_gu"][0]),
        "w_dn1": f(inp["ffn1_w_down"][0]), "w_dn2": f(inp["ffn2_w_down"][0]),
        "w_in": f(inp["w_in"][0]), "w_out": f(inp["w_out"][0]),
        "pool_w": f(inp["pool_w"][0].reshape(512, 128)),
        "small": _small(inp), "gfin": f(inp["final_norm"].reshape(1, D)),
        "ident": np.eye(128, dtype=np.float32),
    }


def kernel(**inputs):
    inp = {k: np.asarray(v) for k, v in inputs.items()}
    xp, xsm = inp["x_prompt"], inp["x_sample"]
    ntiles = 12
    if ntiles not in _NC_CACHE:
        _NC_CACHE[ntiles] = build_nc(ntiles)
    nc = _NC_CACHE[ntiles]
    shared = _shared_maps(inp)
    in_maps = []
    for core in range(8):
        xt, pc = _core_inputs(xp, xsm, core)
        m = dict(shared)
        m["xt"] = xt
        m["pcorr"] = pc
        in_maps.append(m)
    res = run_bass_kernel_spmd(nc, in_maps, core_ids=list(range(8)))
    y_prompt = np.empty_like(xp, dtype=np.float32)
    y_sample = np.empty_like(xsm, dtype=np.float32)
    for core in range(8):
        yt = np.asarray(res.results[core]["yt"]).reshape(12, TT, D)
        y_sample[2 * core] = yt[0:4].reshape(4096, D)
        y_sample[2 * core + 1] = yt[4:8].reshape(4096, D)
        b, q = core // 4, core % 4
        y_prompt[b, q * 4096:(q + 1) * 4096] = yt[8:12].reshape(4096, D)
    return (y_prompt, y_sample)
```

```python
import numpy as np
import concourse.bass as bass
import concourse.mybir as mybir
from concourse.bass_utils import run_bass_kernel_spmd

F32 = mybir.dt.float32
BF16 = mybir.dt.bfloat16
AF = mybir.ActivationFunctionType
ALU = mybir.AluOpType

D = 1024
DFF = 2816
NFC = 22
TT = 1024
HALO = 16
NCOL = TT + 2 * HALO
NCH = 9
NXS = 10
NRING = 8
EPS = 1e-6
GROUPS = [(0, 8), (8, 15), (15, 22)]
PIECES_A = [(0, 512), (512, 1024), (1024, 1056)]
PIECES_B = [(0, 512), (512, 1024)]
NSMALL = 168
ENGS = ["pe", "act", "dve", "pool", "sp"]
NDSEM = {"sp": 12, "pool": 12, "act": 10}


def blocks_of(lo, hi):
    return range(lo // 128, (hi - 1) // 128 + 1)


class Op:
    __slots__ = ("eng", "fn", "deps", "ddeps", "signal", "idx", "dma", "clock")


class Prog:
    def __init__(self):
        self.eng_ops = {e: [] for e in ENGS}
        self.last_writer = {}
        self.readers = {}
        self.known = {e: {} for e in ENGS}
        self.dma_count = {q: [0] * n for q, n in NDSEM.items()}
        self.dma_rr = {q: 0 for q in NDSEM}

    def _collect(self, eng, reads, writes):
        toks = []
        for k in reads:
            w = self.last_writer.get(k)
            if w is not None:
                toks.append(w)
        for k in writes:
            w = self.last_writer.get(k)
            if w is not None:
                toks.append(w)
            toks.extend(self.readers.get(k, ()))
        need = {}
        for t in toks:
            if t[0] == "e" and t[1] == "pe" and eng == "pe":
                continue
            key = (t[0], t[1])
            if need.get(key, -1) < t[2]:
                need[key] = t[2]
        kn = self.known[eng]
        out = {}
        for key, v in need.items():
            if kn.get(key, -1) < v:
                out[key] = v
        return out

    def _learn(self, eng, need):
        kn = self.known[eng]
        for key, v in need.items():
            if kn.get(key, -1) < v:
                kn[key] = v
            if key[0] == "e":
                clk = self.eng_ops[key[1]][v].clock
                for k2, v2 in clk.items():
                    if kn.get(k2, -1) < v2:
                        kn[k2] = v2

    def op(self, eng, fn, reads=(), writes=(), dma=False):
        need = self._collect(eng, reads, writes)
        o = Op()
        o.eng = eng
        o.fn = fn
        o.deps = need
        o.signal = False
        o.dma = None
        o.idx = len(self.eng_ops[eng])
        for key, v in need.items():
            if key[0] == "e":
                self.eng_ops[key[1]][v].signal = True
        self._learn(eng, need)
        o.clock = dict(self.known[eng])
        self.eng_ops[eng].append(o)
        if dma:
            q = eng
            si = self.dma_rr[q]
            self.dma_rr[q] = (si + 1) % NDSEM[q]
            self.dma_count[q][si] += 1
            tok = ("d", (q, si), self.dma_count[q][si] * 16)
            o.dma = (q, si)
        else:
            tok = ("e", eng, o.idx)
        for k in writes:
            self.last_writer[k] = tok
            self.readers[k] = []
        for k in reads:
            self.readers.setdefault(k, []).append(tok)
        return o

    def emit(self, nc, block, esem, dsem, final_waits):
        for e in ENGS:
            ms = 0
            for o in self.eng_ops[e]:
                if o.signal:
                    ms += 1
                    o.signal = ms
        handles = {"pe": block.tensor, "act": block.scalar, "dve": block.vector,
                   "pool": block.gpsimd, "sp": block.sync}
        prog = self

        def make(ename):
            def body(eng):
                for o in prog.eng_ops[ename]:
                    for key, v in o.deps.items():
                        if key[0] == "e":
                            eng.wait_ge(esem[key[1]], prog.eng_ops[key[1]][v].signal)
                        else:
                            eng.wait_ge(dsem[key[1]], v)
                    ins = o.fn(eng)
                    if o.dma is not None:
                        ins.then_inc(dsem[o.dma], 16)
                    elif o.signal:
                        ins.then_inc(esem[ename], 1)
                if ename == "sp":
                    for key, v in final_waits:
                        eng.wait_ge(dsem[key], v)
            return body

        for e in ENGS:
            handles[e](make(e))


def build_nc(ntiles):
    nc = bass.Bass("TRN2", target_bir_lowering=False)
    P = Prog()

    def dram_in(name, shape, dt=F32):
        return nc.dram_tensor(name, list(shape), dt, kind="ExternalInput").ap()

    xt = dram_in("xt", [ntiles, NCOL, D])
    w_gu = [dram_in("w_gu1", [D, 2 * DFF]), dram_in("w_gu2", [D, 2 * DFF])]
    w_dn = [dram_in("w_dn1", [DFF, D]), dram_in("w_dn2", [DFF, D])]
    w_in = dram_in("w_in", [D, 1536])
    w_out = dram_in("w_out", [D, D])
    pool_w = dram_in("pool_w", [512, 128])
    small_d = dram_in("small", [128, NSMALL])
    gfin_d = dram_in("gfin", [1, D])
    pcorr_d = dram_in("pcorr", [1, ntiles * 64])
    ident_d = dram_in("ident", [128, 128])
    yt = nc.dram_tensor("yt", [ntiles, TT, D], F32, kind="ExternalOutput").ap()

    sc_gu = [nc.dram_tensor("sc_gu%d" % i, [NFC, 128, 2048], BF16).ap() for i in (1, 2)]
    sc_dn = [nc.dram_tensor("sc_dn%d" % i, [NFC, 128, D], BF16).ap() for i in (1, 2)]
    sc_in = nc.dram_tensor("sc_in", [6, 128, 2048], BF16).ap()
    sc_out = nc.dram_tensor("sc_out", [8, 128, D], BF16).ap()

    import contextlib
    es = contextlib.ExitStack()
    with es:
        def sb(name, shape, dt):
            return es.enter_context(nc.sbuf_tensor(name, list(shape), dt))

        def ps(name, shape, dt):
            return es.enter_context(nc.psum_tensor(name, list(shape), dt))

        xs = sb("xs", [128, NXS, D], F32)
        hn = sb("hn", [128, 2, D], BF16)
        hb = sb("hb", [128, 8, NCOL], BF16)
        b16 = sb("b16", [128, 8, NCOL], BF16)
        sg = sb("sg", [128, 2, 512], F32)
        upool = sb("upool", [128, 2, NCOL], F32)
        ptmp = sb("ptmp", [128, 2, NCOL], F32)
        ybuf = sb("ybuf", [128, 4, 512], F32)
        ysq = sb("ysq", [128, 2, 512], F32)
        lnt = sb("lnt", [128, 4, 512], F32)
        tnb = sb("tnb", [128, 2, 512], F32)
        diag = sb("diag", [128, 4, 31, 128], BF16)
        ring = sb("ring", [128, NRING, 2048], BF16)
        ostg = sb("ostg", [128, 2, D], F32)
        gfb = sb("gfb_sb", [128, D], F32)
        small = sb("small_sb", [128, NSMALL], F32)
        whalf = sb("whalf", [128, 124], F32)
        pbias = sb("pbias", [128, 4], F32)
        identf = sb("identf", [128, 128], F32)
        identb = sb("identb", [128, 128], BF16)
        onesm = sb("onesm", [128, 128], F32)
        mhalf = sb("mhalf", [128, 1], F32)
        poolw = sb("poolw", [128, 4, 128], BF16)
        pcorr = sb("pcorr_sb", [128, ntiles * 64], F32)
        ss = sb("ss", [128, 16], F32)
        rst = sb("rst", [128, 16], F32)
        sqj = sb("sqj", [128, D], BF16)

        pbank = [ps("pb%d" % i, [128, 512], F32) for i in range(6)]
        tbank = [ps("tb%d" % i, [128, 1024], BF16) for i in range(2)]

        esem = {e: es.enter_context(nc.semaphore("e_" + e)) for e in ENGS}
        dsem = {}
        for q, n in NDSEM.items():
            for i in range(n):
                dsem[(q, i)] = es.enter_context(nc.semaphore("d_%s%d" % (q, i)))

        cnt = {"up": 0, "acc": 0, "tp": 0, "ring": 0, "ssc": 0, "hn": 0, "sg": 0, "ost": 0}

        def nxt(name, mod):
            v = cnt[name]
            cnt[name] = v + 1
            return v % mod

        def dma(q, out_ap, in_ap, reads, writes):
            return P.op(q, lambda e, o=out_ap, i=in_ap: e.dma_start(out=o, in_=i),
                        reads=reads, writes=writes, dma=True)

        def mm_group(out_ap, pairs, reads, writes):
            def fn(e, out_ap=out_ap, pairs=pairs):
                n = len(pairs)
                ins = None
                for i, (l, r) in enumerate(pairs):
                    ins = e.matmul(out_ap, l, r, start=(i == 0), stop=(i == n - 1))
                return ins
            return P.op("pe", fn, reads=reads, writes=writes)

        dma("sp", small[:], small_d, [], [("small",)])
        dma("sp", identf[:], ident_d, [], [("identf",)])
        dma("sp", gfb[:], gfin_d.partition_broadcast(128), [], [("gfb",)])
        dma("sp", pcorr[:], pcorr_d.partition_broadcast(128), [], [("pcorr",)])
        P.op("dve", lambda e: e.tensor_copy(out=identb[:], in_=identf[:]), [("identf",)], [("identb",)])
        P.op("dve", lambda e: e.memset(onesm[:], 1.0 / 512.0), [], [("onesm",)])
        P.op("pool", lambda e: e.memset(mhalf[:], -0.5), [], [("mhalf",)])
        P.op("dve", lambda e: e.tensor_scalar(out=whalf[:], in0=small[:, 44:168], scalar1=0.5, scalar2=None,
                                              op0=ALU.mult), [("small",)], [("whalf",)])
        P.op("dve", lambda e: e.tensor_tensor(out=pbias[:], in0=small[:, 24:28], in1=small[:, 28:32],
                                              op=ALU.mult), [("small",)], [("pbias",)])
        for i in range(4):
            for k in range(31):
                P.op("dve", lambda e, i=i, k=k: e.tensor_scalar(
                    out=diag[:, i, k, :], in0=identf[:], scalar1=whalf[:, i * 31 + k:i * 31 + k + 1],
                    scalar2=None, op0=ALU.mult),
                    [("identf",), ("whalf",)], [("diag", i, k)])

        pro = {"st": 0}

        def stage_unit(srcs, gain_cols, dst_ap, dst_key, shape3):
            s = pro["st"] % (NXS // 2)
            r = pro["st"] % NRING
            pro["st"] += 1
            a, b = shape3
            st32 = xs[:, 2 * s:2 * s + 2, :]
            st32v = st32.rearrange("p s d -> p (s d)").rearrange("p (a b) -> p a b", a=a)
            st16v = ring[:, r, :].rearrange("p (a b) -> p a b", a=a)
            k32 = [("xs", 2 * s), ("xs", 2 * s + 1)]
            for (lo, hi), src in srcs:
                dma("sp", st32v[:, :, lo:hi], src, [], k32)
            if gain_cols is not None:
                g0 = gain_cols
                gap = small[:, g0:g0 + a].unsqueeze(2).to_broadcast([128, a, b])
                P.op("dve", lambda e, o=st16v, i=st32v, g=gap: e.tensor_tensor(out=o, in0=i, in1=g, op=ALU.mult),
                     k32 + [("small",)], [("ring", r)])
            else:
                P.op("dve", lambda e, o=st16v, i=st32v: e.tensor_copy(out=o, in_=i), k32, [("ring", r)])
            dma("act", dst_ap, st16v, [("ring", r)], [dst_key])

        for f in range(2):
            wv = w_gu[f].rearrange("(kc p) f -> p kc f", p=128)
            for j in range(NFC):
                stage_unit([((0, 128), wv[:, :, j * 128:(j + 1) * 128]),
                            ((128, 256), wv[:, :, DFF + j * 128:DFF + (j + 1) * 128])],
                           0 if f == 0 else 16,
                           sc_gu[f][j].rearrange("p (a b) -> p a b", a=8), ("sc_gu", f, j), (8, 256))
            if f == 0:
                wiv = w_in.rearrange("(kc p) f -> p kc f", p=128)
                for u in range(6):
                    if u < 2:
                        srcs = [((0, 256), wiv[:, :, u * 256:(u + 1) * 256])]
                    else:
                        i = u - 2
                        srcs = [((0, 128), wiv[:, :, 512 + i * 128:512 + (i + 1) * 128]),
                                ((128, 256), wiv[:, :, 1024 + i * 128:1024 + (i + 1) * 128])]
                    stage_unit(srcs, 8, sc_in[u].rearrange("p (a b) -> p a b", a=8), ("sc_in", u), (8, 256))
            wdv = w_dn[f].rearrange("(j p) d -> p j d", p=128)
            for q in range(NFC // 2):
                stage_unit([((0, D), wdv[:, 2 * q:2 * q + 2, :])], None,
                           sc_dn[f][2 * q:2 * q + 2].rearrange("j p d -> p j d"), ("sc_dn", f, q), (2, D))
            if f == 0:
                wov = w_out.rearrange("(j p) d -> p j d", p=128)
                for q in range(4):
                    stage_unit([((0, D), wov[:, 2 * q:2 * q + 2, :])], None,
                               sc_out[2 * q:2 * q + 2].rearrange("j p d -> p j d"), ("sc_out", q), (2, D))
        pwst = xs[:, 0, 0:512].rearrange("p (g d) -> p g d", g=4)
        dma("sp", pwst, pool_w.rearrange("(g c) d -> c g d", g=4), [], [("xs", 0)])
        P.op("dve", lambda e: e.tensor_copy(out=poolw[:], in_=pwst), [("xs", 0)], [("poolw",)])

        def load_unit(src_ap, src_key, view=None):
            r = nxt("ring", NRING)
            dst = ring[:, r, :] if view is None else view(ring[:, r, :])
            dma("sp", dst, src_ap, [src_key], [("ring", r)])
            return r

        slotmap = {}
        free_slots = list(range(NXS))

        def xslot(ti, c):
            return slotmap[(ti, c)]

        def free_slot(ti, c):
            free_slots.append(slotmap[(ti, c)])

        def rms_stats(slot, rows):
            col = nxt("ssc", 16)
            P.op("act", lambda e: e.activation(out=sqj[:rows, :], in_=xs[:rows, slot, :], func=AF.Square,
                                               accum_out=ss[:rows, col:col + 1]),
                 [("xs", slot)], [("sqj",), ("ss", col)])
            P.op("pool", lambda e: e.tensor_scalar(out=ss[:rows, col:col + 1], in0=ss[:rows, col:col + 1],
                                                   scalar1=1.0 / D, scalar2=EPS, op0=ALU.mult, op1=ALU.add),
                 [("ss", col)], [("ss", col)])
            P.op("pool", lambda e: e.tensor_tensor(out=rst[:rows, col:col + 1], in0=ss[:rows, col:col + 1],
                                                   in1=mhalf[:rows, :], op=ALU.pow),
                 [("ss", col), ("mhalf",)], [("rst", col)])
            return col

        def norm_scale(slot, rows, col):
            h = nxt("hn", 2)
            P.op("act", lambda e: e.activation(out=hn[:rows, h, :], in_=xs[:rows, slot, :], func=AF.Copy,
                                               scale=rst[:rows, col:col + 1]),
                 [("xs", slot), ("rst", col)], [("hn", h)])
            return h

        def norm_transpose(h, c):
            rows = 128 if c < 8 else 32
            t = nxt("tp", 2)
            tb = tbank[t]

            def fn(e):
                ins = None
                for kc in range(8):
                    ins = e.transpose(tb[:, kc * 128:kc * 128 + rows], hn[:rows, h, kc * 128:(kc + 1) * 128],
                                      identb[:rows, :rows])
                return ins
            P.op("pe", fn, [("hn", h), ("identb",)], [("tb", t)])
            src = tb[:, :].rearrange("p (k t) -> p k t", k=8)[:, :, 0:rows]
            lo = c * 128
            P.op("act", lambda e: e.activation(out=hb[:, :, lo:lo + rows], in_=src, func=AF.Copy),
                 [("tb", t)], [("hb", kc, c) for kc in range(8)])

        def ffn(ti, f, nchunks, pieces, before_last_down=None, on_chunk=None):
            for gi, (j0, j1) in enumerate(GROUPS):
                last = gi == len(GROUPS) - 1
                for j in range(j0, j1):
                    r = load_unit(sc_gu[f][j], ("sc_gu", f, j))
                    wu = ring[:, r, :].rearrange("p (k c) -> p k c", k=8)
                    jj = j - j0
                    for (lo, hi) in pieces:
                        n = hi - lo
                        u = nxt("up", 2)
                        pg, pu = pbank[2 * u], pbank[2 * u + 1]
                        hkeys = [("hb", kc, b) for kc in range(8) for b in blocks_of(lo, hi)]
                        mm_group(pg[:, :n], [(wu[:, kc, 0:128], hb[:, kc, lo:hi]) for kc in range(8)],
                                 hkeys + [("ring", r)], [("pb", 2 * u)])
                        mm_group(pu[:, :n], [(wu[:, kc, 128:256], hb[:, kc, lo:hi]) for kc in range(8)],
                                 hkeys + [("ring", r)], [("pb", 2 * u + 1)])
                        s = nxt("sg", 2)
                        P.op("act", lambda e, pg=pg, n=n, s=s: e.activation(out=sg[:, s, :n], in_=pg[:, :n], func=AF.Silu),
                             [("pb", 2 * u)], [("sg", s)])
                        P.op("dve", lambda e, pu=pu, n=n, s=s, jj=jj, lo=lo, hi=hi: e.tensor_tensor(
                            out=b16[:, jj, lo:hi], in0=sg[:, s, :n], in1=pu[:, :n], op=ALU.mult),
                            [("sg", s), ("pb", 2 * u + 1)], [("b16", jj, b) for b in blocks_of(lo, hi)])
                nj = j1 - j0
                slots = []
                jq = j0
                while jq < j1:
                    if jq % 2 == 0 and jq + 1 < j1:
                        r = load_unit(sc_dn[f][jq:jq + 2].rearrange("j p d -> p j d"), ("sc_dn", f, jq // 2),
                                      view=lambda a: a.rearrange("p (j d) -> p j d", j=2))
                        slots.append((r, 0))
                        slots.append((r, 1))
                        jq += 2
                    else:
                        r = load_unit(sc_dn[f][jq], ("sc_dn", f, jq // 2),
                                      view=lambda a: a[:, 0:D])
                        slots.append((r, 0))
                        jq += 1
                if last and before_last_down is not None:
                    before_last_down()
                for c in range(nchunks):
                    rows = 128 if c < 8 else 32
                    lo = c * 128
                    slot = xslot(ti, c)
                    for half in range(2):
                        a = 4 + nxt("acc", 2)
                        pa = pbank[a]
                        pairs = []
                        for jj in range(nj):
                            r, o = slots[jj]
                            pairs.append((b16[:, jj, lo:lo + rows],
                                          ring[:, r, o * D + half * 512:o * D + (half + 1) * 512]))
                        mm_group(pa[:rows, :], pairs,
                                 [("b16", jj, c) for jj in range(nj)] + [("ring", r) for r, _ in slots],
                                 [("pb", a)])
                        P.op("dve", lambda e, pa=pa, rows=rows, slot=slot, half=half: e.scalar_tensor_tensor(
                            out=xs[:rows, slot, half * 512:(half + 1) * 512], in0=pa[:rows, :], scalar=0.5,
                            in1=xs[:rows, slot, half * 512:(half + 1) * 512], op0=ALU.mult, op1=ALU.add),
                            [("pb", a), ("xs", slot)], [("xs", slot)])
                    if last and on_chunk is not None:
                        on_chunk(c)

        pending_T = []

        def norm_pre(ti, c):
            rows = 128 if c < 8 else 32
            slot = xslot(ti, c)
            col = rms_stats(slot, rows)
            h = norm_scale(slot, rows, col)
            pending_T.append((h, c))

        def flush_T(keep=0):
            while len(pending_T) > keep:
                h, c = pending_T.pop(0)
                norm_transpose(h, c)

        def norm_all(ti, nchunks):
            for c in range(nchunks):
                norm_pre(ti, c)
                flush_T(keep=1)
            flush_T(0)

        def tcols(lo, hi):
            if lo < 1024:
                return [(0, hi - lo, 16 + lo)]
            return [(0, 16, 0), (16, 32, 1040)]

        def mixer(ti):
            for u in range(2):
                r = load_unit(sc_in[u], ("sc_in", u))
                wu = ring[:, r, :].rearrange("p (k c) -> p k c", k=8)
                for gg in range(2):
                    g = 2 * u + gg
                    ur = g % 2
                    for (lo, hi) in PIECES_A:
                        n = hi - lo
                        a = 4 + nxt("acc", 2)
                        pa = pbank[a]
                        hkeys = [("hb", kc, b) for kc in range(8) for b in blocks_of(lo, hi)]
                        mm_group(pa[:, :n], [(wu[:, kc, gg * 128:(gg + 1) * 128], hb[:, kc, lo:hi]) for kc in range(8)],
                                 hkeys + [("ring", r)], [("pb", a)])
                        for (p0, p1, t0) in tcols(lo, hi):
                            P.op("act", lambda e, pa=pa, p0=p0, p1=p1, t0=t0, ur=ur: e.activation(
                                out=upool[:, ur, t0:t0 + (p1 - p0)], in_=pa[:, p0:p1], func=AF.Copy),
                                [("pb", a)], [("upool", ur, b) for b in blocks_of(t0, t0 + p1 - p0)])
                    U = upool[:, ur, :]
                    ukeys = [("upool", ur, b) for b in range(9)]
                    w = 2 ** (g + 1)
                    hlf = w // 2
                    cur = U
                    curkeys = ukeys
                    ext = 1056
                    step = 1
                    for lvl in range(g + 1):
                        ext2 = ext - step
                        dst = ptmp[:, lvl % 2, :]
                        P.op("dve", lambda e, cur=cur, dst=dst, ext2=ext2, step=step: e.tensor_tensor(
                            out=dst[:, 0:ext2], in0=cur[:, 0:ext2], in1=cur[:, step:step + ext2], op=ALU.add),
                            curkeys, [("ptmp", lvl % 2)])
                        cur = dst
                        curkeys = [("ptmp", lvl % 2)]
                        ext = ext2
                        step *= 2
                    sidx = g % 2
                    other = ptmp[:, (g + 1) % 2, :]
                    P.op("dve", lambda e, cur=cur, other=other, hlf=hlf, w=w: e.tensor_scalar(
                        out=other[:, 0:TT], in0=cur[:, 16 - hlf:16 - hlf + TT], scalar1=1.0 / w, scalar2=None,
                        op0=ALU.mult), curkeys, [("ptmp", (g + 1) % 2)])
                    okeys = [("ptmp", (g + 1) % 2)]
                    cb = ti * 64 + g * 16
                    P.op("dve", lambda e, other=other, cb=cb: e.tensor_tensor(
                        out=other[:, 0:8], in0=other[:, 0:8], in1=pcorr[:, cb:cb + 8], op=ALU.mult),
                        okeys + [("pcorr",)], okeys)
                    P.op("dve", lambda e, other=other, cb=cb: e.tensor_tensor(
                        out=other[:, TT - 8:TT], in0=other[:, TT - 8:TT], in1=pcorr[:, cb + 8:cb + 16], op=ALU.mult),
                        okeys + [("pcorr",)], okeys)
                    prow = 4 + g
                    P.op("dve", lambda e, other=other, U=U, prow=prow: e.tensor_tensor(
                        out=b16[:, prow, 0:TT], in0=other[:, 0:TT], in1=U[:, 16:16 + TT], op=ALU.subtract),
                        okeys + ukeys, [("b16", prow, b) for b in range(8)])
            for i in range(4):
                r = load_unit(sc_in[2 + i], ("sc_in", 2 + i))
                wu = ring[:, r, :].rearrange("p (k c) -> p k c", k=8)
                for (lo, hi) in PIECES_A:
                    n = hi - lo
                    u = nxt("up", 2)
                    pv, pgt = pbank[2 * u], pbank[2 * u + 1]
                    hkeys = [("hb", kc, b) for kc in range(8) for b in blocks_of(lo, hi)]
                    mm_group(pv[:, :n], [(wu[:, kc, 0:128], hb[:, kc, lo:hi]) for kc in range(8)],
                             hkeys + [("ring", r)], [("pb", 2 * u)])
                    mm_group(pgt[:, :n], [(wu[:, kc, 128:256], hb[:, kc, lo:hi]) for kc in range(8)],
                             hkeys + [("ring", r)], [("pb", 2 * u + 1)])
                    s = nxt("sg", 2)
                    P.op("act", lambda e, pgt=pgt, n=n, s=s: e.activation(out=sg[:, s, :n], in_=pgt[:, :n],
                                                                          func=AF.Tanh, scale=0.5),
                         [("pb", 2 * u + 1)], [("sg", s)])
                    for (p0, p1, t0) in tcols(lo, hi):
                        P.op("dve", lambda e, pv=pv, s=s, p0=p0, p1=p1, t0=t0, i=i: e.scalar_tensor_tensor(
                            out=b16[:, i, t0:t0 + (p1 - p0)], in0=sg[:, s, p0:p1], scalar=1.0, in1=pv[:, p0:p1],
                            op0=ALU.add, op1=ALU.mult),
                            [("sg", s), ("pb", 2 * u)], [("b16", i, b) for b in blocks_of(t0, t0 + p1 - p0)])

        pend_pool = []

        def mixer_tail(ti):
            for g in range(4):
              for (lo, hi) in PIECES_B:
                a = 4 + nxt("acc", 2)
                pa = pbank[a]
                prow = 4 + g
                mm_group(pa[:, :], [(poolw[:, g, :], b16[:, prow, lo:hi])],
                         [("b16", prow, b) for b in blocks_of(lo, hi)] + [("poolw",)], [("pb", a)])
                P.op("act", lambda e, pa=pa, g=g, lo=lo, hi=hi: e.activation(
                    out=hb[:, g, lo:hi], in_=pa[:, :], func=AF.Identity,
                    scale=small[:, 28 + g:29 + g], bias=pbias[:, g:g + 1]),
                    [("pb", a), ("small",), ("pbias",)], [("hb", g, b) for b in blocks_of(lo, hi)])
            for (lo, hi) in PIECES_B:
                pm, pe2 = pbank[0], pbank[1]
                for i in range(4):
                    a = 4 + nxt("acc", 2)
                    pa = pbank[a]
                    pairs = [(diag[:, i, k, :], b16[:, i, lo + k + 1:lo + k + 1 + 512]) for k in range(31)]
                    mm_group(pa[:, :], pairs,
                             [("b16", i, b) for b in blocks_of(lo + 1, lo + 31 + 512)] +
                             [("diag", i, k) for k in range(31)], [("pb", a)])
                    P.op("act", lambda e, pa=pa, i=i: e.activation(out=ybuf[:, i, :], in_=pa[:, :], func=AF.Identity,
                                                                   bias=small[:, 32 + i:33 + i]),
                         [("pb", a), ("small",)], [("ybuf", i)])
                    q = i % 2
                    P.op("act", lambda e, pa=pa, i=i, q=q: e.activation(out=ysq[:, q, :], in_=pa[:, :], func=AF.Square,
                                                                        bias=small[:, 32 + i:33 + i]),
                         [("pb", a), ("small",)], [("ysq", q)])
                    P.op("pe", lambda e, i=i: e.matmul(pm[:, :], onesm[:, :], ybuf[:, i, :], start=(i == 0), stop=(i == 3)),
                         [("ybuf", i), ("onesm",)], [("pb", 0)])
                    P.op("pe", lambda e, i=i, q=q: e.matmul(pe2[:, :], onesm[:, :], ysq[:, q, :], start=(i == 0), stop=(i == 3)),
                         [("ysq", q), ("onesm",)], [("pb", 1)])
                P.op("act", lambda e: e.activation(out=lnt[:, 0, :], in_=pm[:, :], func=AF.Copy), [("pb", 0)], [("lnt", 0)])
                P.op("act", lambda e: e.activation(out=lnt[:, 1, :], in_=pm[:, :], func=AF.Square), [("pb", 0)], [("lnt", 1)])
                P.op("dve", lambda e: e.scalar_tensor_tensor(out=lnt[:, 2, :], in0=pe2[:, :], scalar=EPS, in1=lnt[:, 1, :],
                                                             op0=ALU.add, op1=ALU.subtract),
                     [("pb", 1), ("lnt", 1)], [("lnt", 2)])
                P.op("act", lambda e: e.activation(out=lnt[:, 2, :], in_=lnt[:, 2, :], func=AF.Sqrt),
                     [("lnt", 2)], [("lnt", 2)])
                P.op("dve", lambda e: e.reciprocal(out=lnt[:, 3, :], in_=lnt[:, 2, :]),
                     [("lnt", 2)], [("lnt", 3)])
                for i in range(4):
                    q = i % 2
                    P.op("dve", lambda e, i=i, q=q: e.tensor_tensor(out=tnb[:, q, :], in0=ybuf[:, i, :], in1=lnt[:, 0, :],
                                                                    op=ALU.subtract),
                         [("ybuf", i), ("lnt", 0)], [("tnb", q)])
                    P.op("dve", lambda e, q=q: e.tensor_tensor(out=tnb[:, q, :], in0=tnb[:, q, :], in1=lnt[:, 3, :],
                                                               op=ALU.mult),
                         [("tnb", q), ("lnt", 3)], [("tnb", q)])
                    P.op("act", lambda e, i=i, q=q, lo=lo, hi=hi: e.activation(
                        out=hb[:, 4 + i, lo:hi], in_=tnb[:, q, :], func=AF.Silu,
                        scale=small[:, 36 + i:37 + i], bias=small[:, 40 + i:41 + i]),
                        [("tnb", q), ("small",)], [("hb", 4 + i, b) for b in blocks_of(lo, hi)])
            slots = []
            for q in range(4):
                r = load_unit(sc_out[2 * q:2 * q + 2].rearrange("j p d -> p j d"), ("sc_out", q),
                              view=lambda a: a.rearrange("p (j d) -> p j d", j=2))
                slots += [(r, 0), (r, 1)]
            for c in range(8):
                lo = c * 128
                slot = xslot(ti, c)
                for half in range(2):
                    a = 4 + nxt("acc", 2)
                    pa = pbank[a]
                    pairs = [(hb[:, ci, lo:lo + 128],
                              ring[:, slots[ci][0], slots[ci][1] * D + half * 512:slots[ci][1] * D + (half + 1) * 512])
                             for ci in range(8)]
                    mm_group(pa[:, :], pairs, [("hb", ci, c) for ci in range(8)] + [("ring", r) for r, _ in slots],
                             [("pb", a)])
                    P.op("dve", lambda e, pa=pa, slot=slot, half=half: e.tensor_tensor(
                        out=xs[:, slot, half * 512:(half + 1) * 512], in0=pa[:, :],
                        in1=xs[:, slot, half * 512:(half + 1) * 512], op=ALU.add),
                        [("pb", a), ("xs", slot)], [("xs", slot)])
                norm_pre(ti, c)
                flush_T(keep=1)
            flush_T(0)

        def load_chunk(ti, c):
            slot = free_slots.pop(0)
            slotmap[(ti, c)] = slot
            if c < 8:
                dma("pool", xs[:, slot, :], xt[ti, c * 128:(c + 1) * 128, :], [], [("xs", slot)])
            else:
                dma("pool", xs[:32, slot, :], xt[ti, TT:TT + 32, :], [], [("xs", slot)])

        nxt_state = {"to_load": [], "to_norm": []}

        def load_more(ti):
            while free_slots and nxt_state["to_load"]:
                c = nxt_state["to_load"].pop(0)
                load_chunk(ti, c)
                nxt_state["to_norm"].append(c)

        def norm_more(ti):
            while nxt_state["to_norm"] and len(pending_T) < 2:
                norm_pre(ti, nxt_state["to_norm"].pop(0))

        def final_norm(ti, c):
            slot = xslot(ti, c)
            col = rms_stats(slot, 128)
            o = nxt("ost", 2)
            P.op("dve", lambda e, slot=slot, col=col, o=o: e.scalar_tensor_tensor(
                out=ostg[:, o, :], in0=xs[:, slot, :], scalar=rst[:, col:col + 1], in1=gfb[:, :],
                op0=ALU.mult, op1=ALU.mult),
                [("xs", slot), ("rst", col), ("gfb",)], [("ostg", o)])
            dma("pool", yt[ti, c * 128:(c + 1) * 128, :], ostg[:, o, :], [("ostg", o)], [("yt", ti, c)])
            free_slot(ti, c)

        for c in range(NCH):
            load_chunk(0, c)
        norm_all(0, NCH)
        for ti in range(ntiles):
            has_next = ti + 1 < ntiles

            def mix_cb(c, ti=ti):
                norm_pre(ti, c)
                if c == 8:
                    free_slot(ti, 8)
                flush_T(keep=1)

            ffn(ti, 0, NCH, PIECES_A, on_chunk=mix_cb)
            flush_T(0)
            mixer(ti)
            mixer_tail(ti)

            def pre_last(ti=ti, has_next=has_next):
                if has_next:
                    nxt_state["to_load"] = list(range(NCH))
                    nxt_state["to_norm"] = []
                    load_more(ti + 1)
                    norm_more(ti + 1)

            def fin_cb(c, ti=ti, has_next=has_next):
                final_norm(ti, c)
                if has_next:
                    load_more(ti + 1)
                    flush_T(keep=1)
                    norm_more(ti + 1)

            ffn(ti, 1, 8, PIECES_B, before_last_down=pre_last, on_chunk=fin_cb)
            if has_next:
                load_more(ti + 1)
                while nxt_state["to_norm"] or pending_T:
                    flush_T(keep=1 if nxt_state["to_norm"] else 0)
                    norm_more(ti + 1)

        with nc.Block() as block:
            fw = [(("pool", i), P.dma_count["pool"][i] * 16) for i in range(NDSEM["pool"]) if P.dma_count["pool"][i] > 0]
            P.emit(nc, block, esem, dsem, fw)
    return nc


def _core_inputs(x_prompt, x_sample, core, ntiles=12):
    segs = []
    for s in range(2):
        segs.append((x_sample[2 * core + s], 0, 4096))
    b, q = core // 4, core % 4
    segs.append((x_prompt[b], q * 4096, (q + 1) * 4096))
    xt = np.zeros((ntiles, NCOL, D), np.float32)
    pc = np.ones((ntiles, 4, 16), np.float32)
    ti = 0
    for (seq, s0, s1) in segs:
        L = seq.shape[0]
        for i in range((s1 - s0) // TT):
            if ti >= ntiles:
                break
            t0 = s0 + i * TT
            xt[ti, :TT] = seq[t0:t0 + TT]
            if t0 - HALO >= 0:
                xt[ti, TT:TT + HALO] = seq[t0 - HALO:t0]
            if t0 + TT + HALO <= L:
                xt[ti, TT + HALO:] = seq[t0 + TT:t0 + TT + HALO]
            for g in range(4):
                w = 2 ** (g + 1)
                h = w // 2
                for m in range(8):
                    t = t0 + m
                    c = min(t + h, L) - max(t - h, 0)
                    pc[ti, g, m] = w / c
                    t = t0 + TT - 8 + m
                    c = min(t + h, L) - max(t - h, 0)
                    pc[ti, g, 8 + m] = w / c
            ti += 1
    return xt, pc.reshape(1, -1)


def _small(inp):
    sm = np.zeros((128, NSMALL), np.float32)
    sm[:, 0:8] = inp["ffn1_norm"][0].reshape(8, 128).T
    sm[:, 8:16] = inp["mix_norm"][0].reshape(8, 128).T
    sm[:, 16:24] = inp["ffn2_norm"][0].reshape(8, 128).T
    sm[:, 24:28] = inp["pool_b"][0].T
    sm[:, 28:32] = inp["pool_scale"][0].reshape(4, 128).T
    sm[:, 32:36] = inp["dw_b"][0].reshape(4, 128).T
    sm[:, 36:40] = inp["conv_ln_g"][0].reshape(4, 128).T
    sm[:, 40:44] = inp["conv_ln_b"][0].reshape(4, 128).T
    dw = inp["dw_w"][0]
    sm[:, 44:168] = dw.reshape(31, 4, 128).transpose(2, 1, 0).reshape(128, 124)
    return sm


_NC_CACHE = {}


def _shared_maps(inp):
    f = lambda a: np.ascontiguousarray(np.asarray(a, dtype=np.float32))
    return {
        "w_gu1": f(inp["ffn1_w_gu"][0]), "w_gu2": f(inp["ffn2_w_gu"][0]),
        "w_dn1": f(inp["ffn1_w_down"][0]), "w_dn2": f(inp["ffn2_w_down"][0]),
        "w_in": f(inp["w_in"][0]), "w_out": f(inp["w_out"][0]),
        "pool_w": f(inp["pool_w"][0].reshape(512, 128)),
        "small": _small(inp), "gfin": f(inp["final_norm"].reshape(1, D)),
        "ident": np.eye(128, dtype=np.float32),
    }


def kernel(**inputs):
    inp = {k: np.asarray(v) for k, v in inputs.items()}
    xp, xsm = inp["x_prompt"], inp["x_sample"]
    ntiles = 12
    if ntiles not in _NC_CACHE:
        _NC_CACHE[ntiles] = build_nc(ntiles)
    nc = _NC_CACHE[ntiles]
    shared = _shared_maps(inp)
    in_maps = []
    for core in range(8):
        xt, pc = _core_inputs(xp, xsm, core)
        m = dict(shared)
        m["xt"] = xt
        m["pcorr"] = pc
        in_maps.append(m)
    res = run_bass_kernel_spmd(nc, in_maps, core_ids=list(range(8)))
    y_prompt = np.empty_like(xp, dtype=np.float32)
    y_sample = np.empty_like(xsm, dtype=np.float32)
    for core in range(8):
        yt = np.asarray(res.results[core]["yt"]).reshape(12, TT, D)
        y_sample[2 * core] = yt[0:4].reshape(4096, D)
        y_sample[2 * core + 1] = yt[4:8].reshape(4096, D)
        b, q = core // 4, core % 4
        y_prompt[b, q * 4096:(q + 1) * 4096] = yt[8:12].reshape(4096, D)
    return (y_prompt, y_sample)
```

```python
import numpy as np
import concourse.bass as bass
import concourse.mybir as mybir
from concourse.bass_utils import run_bass_kernel_spmd

F32 = mybir.dt.float32
BF16 = mybir.dt.bfloat16
AF = mybir.ActivationFunctionType
ALU = mybir.AluOpType

D = 1024
DFF = 2816
NFC = 22
TT = 1024
HALO = 16
NCOL = TT + 2 * HALO
NCH = 9
NXS = 12
NRING = 8
EPS = 1e-6
GROUPS = [(0, 7), (7, 14), (14, 22)]
PIECES_A = [(0, 512), (512, 1024), (1024, 1056)]
PIECES_B = [(0, 512), (512, 1024)]
NSMALL = 168
ENGS = ["pe", "act", "dve", "pool", "sp"]
NDSEM = {"sp": 12, "pool": 12, "act": 10}
ATTACH_WAITS = True


def blocks_of(lo, hi):
    return range(lo // 128, (hi - 1) // 128 + 1)


class Op:
    __slots__ = ("eng", "fn", "deps", "ddeps", "signal", "idx", "dma", "clock", "label", "ninst")


class Prog:
    def __init__(self):
        self.eng_ops = {e: [] for e in ENGS}
        self.last_writer = {}
        self.readers = {}
        self.known = {e: {} for e in ENGS}
        self.dma_count = {q: [0] * n for q, n in NDSEM.items()}
        self.dma_rr = {q: 0 for q in NDSEM}
        self.phase = ""

    def _collect(self, eng, reads, writes):
        toks = []
        for k in reads:
            w = self.last_writer.get(k)
            if w is not None:
                toks.append(w)
        for k in writes:
            w = self.last_writer.get(k)
            if w is not None:
                toks.append(w)
            toks.extend(self.readers.get(k, ()))
        need = {}
        for t in toks:
            if t[0] == "e" and t[1] == "pe" and eng == "pe":
                continue
            key = (t[0], t[1])
            if need.get(key, -1) < t[2]:
                need[key] = t[2]
        kn = self.known[eng]
        out = {}
        for key, v in need.items():
            if kn.get(key, -1) < v:
                out[key] = v
        return out

    def _learn(self, eng, need):
        kn = self.known[eng]
        for key, v in need.items():
            if kn.get(key, -1) < v:
                kn[key] = v
            if key[0] == "e":
                clk = self.eng_ops[key[1]][v].clock
                for k2, v2 in clk.items():
                    if kn.get(k2, -1) < v2:
                        kn[k2] = v2

    def op(self, eng, fn, reads=(), writes=(), dma=False):
        need = self._collect(eng, reads, writes)
        if dma:
            si0 = self.dma_rr[eng]
            prev = self.dma_count[eng][si0] * 16
            if prev > 0 and self.known[eng].get(("d", (eng, si0)), -1) < prev:
                need[("d", (eng, si0))] = max(need.get(("d", (eng, si0)), -1), prev)
        o = Op()
        o.eng = eng
        o.fn = fn
        o.deps = need
        o.signal = False
        o.dma = None
        o.idx = len(self.eng_ops[eng])
        o.label = self.phase
        o.ninst = 1
        for key, v in need.items():
            if key[0] == "e":
                self.eng_ops[key[1]][v].signal = True
        self._learn(eng, need)
        o.clock = dict(self.known[eng])
        self.eng_ops[eng].append(o)
        if dma:
            q = eng
            si = self.dma_rr[q]
            self.dma_rr[q] = (si + 1) % NDSEM[q]
            self.dma_count[q][si] += 1
            tok = ("d", (q, si), self.dma_count[q][si] * 16)
            o.dma = (q, si)
        else:
            tok = ("e", eng, o.idx)
        for k in writes:
            self.last_writer[k] = tok
            self.readers[k] = []
        for k in reads:
            self.readers.setdefault(k, []).append(tok)
        return o

    def emit(self, nc, block, esem, dsem, final_waits):
        for e in ENGS:
            ms = 0
            for o in self.eng_ops[e]:
                if o.signal:
                    ms += 1
                    o.signal = ms
        handles = {"pe": block.tensor, "act": block.scalar, "dve": block.vector,
                   "pool": block.gpsimd, "sp": block.sync}
        prog = self

        def make(ename):
            def body(eng):
                for o in prog.eng_ops[ename]:
                    deps = []
                    for key, v in o.deps.items():
                        if key[0] == "e":
                            deps.append((esem[key[1]], prog.eng_ops[key[1]][v].signal))
                        else:
                            deps.append((dsem[key[1]], v))
                    attach = None
                    if deps and ATTACH_WAITS and o.dma is None and ename in ("pe", "act", "dve"):
                        attach = deps.pop()
                    for sem, v in deps:
                        eng.wait_ge(sem, v)
                    ins = o.fn(eng)
                    first = ins
                    if isinstance(ins, tuple):
                        first, ins = ins
                    if attach is not None:
                        first._wait_ge(attach[0], attach[1])
                    if o.dma is not None:
                        ins.then_inc(dsem[o.dma], 16)
                    elif o.signal:
                        ins.then_inc(esem[ename], 1)
                if ename == "sp":
                    for key, v in final_waits:
                        eng.wait_ge(dsem[key], v)
            return body

        for e in ENGS:
            handles[e](make(e))


def build_nc(ntiles):
    nc = bass.Bass("TRN2", target_bir_lowering=False)
    P = Prog()

    def dram_in(name, shape, dt=F32):
        return nc.dram_tensor(name, list(shape), dt, kind="ExternalInput").ap()

    xt = dram_in("xt", [ntiles, NCOL, D])
    w_gu = [dram_in("w_gu1", [D, 2 * DFF]), dram_in("w_gu2", [D, 2 * DFF])]
    w_dn = [dram_in("w_dn1", [DFF, D]), dram_in("w_dn2", [DFF, D])]
    w_in = dram_in("w_in", [D, 1536])
    w_out = dram_in("w_out", [D, D])
    pool_w = dram_in("pool_w", [512, 128])
    small_d = dram_in("small", [128, NSMALL])
    gfin_d = dram_in("gfin", [1, D])
    pcorr_d = dram_in("pcorr", [1, ntiles * 64])
    ident_d = dram_in("ident", [128, 128])
    yt = nc.dram_tensor("yt", [ntiles, TT, D], F32, kind="ExternalOutput").ap()

    sc_gu = [nc.dram_tensor("sc_gu%d" % i, [NFC, 128, 2048], BF16).ap() for i in (1, 2)]
    sc_dn = [nc.dram_tensor("sc_dn%d" % i, [NFC, 128, D], BF16).ap() for i in (1, 2)]
    sc_in = nc.dram_tensor("sc_in", [6, 128, 2048], BF16).ap()
    sc_out = nc.dram_tensor("sc_out", [8, 128, D], BF16).ap()

    import contextlib
    es = contextlib.ExitStack()
    with es:
        def sb(name, shape, dt):
            return es.enter_context(nc.sbuf_tensor(name, list(shape), dt))

        def ps(name, shape, dt):
            return es.enter_context(nc.psum_tensor(name, list(shape), dt))

        xs = sb("xs", [128, NXS, D], F32)
        hn = sb("hn", [128, 3, D], BF16)
        hb = sb("hb", [128, 8, NCOL], BF16)
        b16 = sb("b16", [128, 8, NCOL], BF16)
        sg = sb("sg", [128, 2, 512], F32)
        upool = sb("upool", [128, 2, NCOL], F32)
        ptmp = sb("ptmp", [128, 2, NCOL], F32)
        ybuf = sb("ybuf", [128, 2, 4, 512], F32)
        ysq = sb("ysq", [128, 2, 2, 512], BF16)
        lnt = sb("lnt", [128, 2, 1024], F32)
        diag = sb("diag", [128, 4, 31, 128], BF16)
        ring = sb("ring", [128, NRING, 2048], BF16)
        gfb = sb("gfb_sb", [128, D], F32)
        small = sb("small_sb", [128, NSMALL], F32)
        whalf = sb("whalf", [128, 124], F32)
        pbias = sb("pbias", [128, 4], F32)
        identf = sb("identf", [128, 128], F32)
        identb = sb("identb", [128, 128], BF16)
        onesm = sb("onesm", [128, 128], F32)
        onesb = sb("onesb", [128, 128], BF16)
        mhalf = sb("mhalf", [128, 1], F32)
        poolw = sb("poolw", [128, 4, 128], BF16)
        pcorr = sb("pcorr_sb", [128, 2, 64], F32)
        ss = sb("ss", [128, 16], F32)
        rst = sb("rst", [128, 16], F32)

        pbank = [ps("pb%d" % i, [128, 512], F32) for i in range(6)]
        tbank = [ps("tb%d" % i, [128, 1024], BF16) for i in range(2)]

        esem = {e: es.enter_context(nc.semaphore("e_" + e)) for e in ENGS}
        dsem = {}
        for q, n in NDSEM.items():
            for i in range(n):
                dsem[(q, i)] = es.enter_context(nc.semaphore("d_%s%d" % (q, i)))

        cnt = {"up": 0, "acc": 0, "tp": 0, "ring": 0, "ssc": 0, "hn": 0, "sg": 0, "ost": 0, "nsc": 0, "dacc": 0, "up3": 0}

        def nxt(name, mod):
            v = cnt[name]
            cnt[name] = v + 1
            return v % mod

        def dma(q, out_ap, in_ap, reads, writes):
            return P.op(q, lambda e, o=out_ap, i=in_ap: e.dma_start(out=o, in_=i),
                        reads=reads, writes=writes, dma=True)

        def mm_group(out_ap, pairs, reads, writes):
            def fn(e, out_ap=out_ap, pairs=pairs):
                n = len(pairs)
                ins = None
                first = None
                for i, (l, r) in enumerate(pairs):
                    ins = e.matmul(out_ap, l, r, start=(i == 0), stop=(i == n - 1))
                    if first is None:
                        first = ins
                return (first, ins)
            o = P.op("pe", fn, reads=reads, writes=writes)
            o.ninst = len(pairs)
            return o

        dma("sp", small[:], small_d, [], [("small",)])
        dma("sp", identf[:], ident_d, [], [("identf",)])
        dma("sp", gfb[:], gfin_d.partition_broadcast(128), [], [("gfb",)])
        P.op("dve", lambda e: e.tensor_copy(out=identb[:], in_=identf[:]), [("identf",)], [("identb",)])
        P.op("dve", lambda e: e.memset(onesm[:], 1.0 / 512.0), [], [("onesm",)])
        P.op("dve", lambda e: e.memset(onesb[:], 1.0 / 512.0), [], [("onesb",)])
        P.op("pool", lambda e: e.memset(mhalf[:], -0.5), [], [("mhalf",)])
        P.op("dve", lambda e: e.tensor_scalar(out=whalf[:], in0=small[:, 44:168], scalar1=0.5, scalar2=None,
                                              op0=ALU.mult), [("small",)], [("whalf",)])
        P.op("dve", lambda e: e.tensor_tensor(out=pbias[:], in0=small[:, 24:28], in1=small[:, 28:32],
                                              op=ALU.mult), [("small",)], [("pbias",)])
        bg_work = []
        for i in range(4):
            for k in range(31):
                bg_work.append((i, k))

        def bg_step(n):
            for _ in range(n):
                if not bg_work:
                    return
                i, k = bg_work.pop(0)
                P.op("dve", lambda e, i=i, k=k: e.tensor_scalar(
                    out=diag[:, i, k, :], in0=identf[:], scalar1=whalf[:, i * 31 + k:i * 31 + k + 1],
                    scalar2=None, op0=ALU.mult),
                    [("identf",), ("whalf",)], [("diag", i, k)])

        pwst = xs[:, 0, 0:512].rearrange("p (g d) -> p g d", g=4)
        dma("sp", pwst, pool_w.rearrange("(g c) d -> c g d", g=4), [], [("xs", 0)])
        P.op("dve", lambda e: e.tensor_copy(out=poolw[:], in_=pwst), [("xs", 0)], [("poolw",)])

        jobs = []
        job_index = {}

        def add_job(key, srcs, a_, b_, gain, store_ap):
            job_index[key] = len(jobs)
            jobs.append(dict(key=key, srcs=srcs, a=a_, b=b_, gain=gain, store=store_ap))

        def dn_loads(j0, j1):
            out = []
            jq = j0
            while jq < j1:
                if jq % 2 == 0 and jq + 1 < j1:
                    out.append((jq, 2))
                    jq += 2
                else:
                    out.append((jq, 1))
                    jq += 1
            return out

        def plan_ffn(f):
            wv = w_gu[f].rearrange("(kc p) f -> p kc f", p=128)
            wdv = w_dn[f].rearrange("(j p) d -> p j d", p=128)
            for (j0, j1) in GROUPS:
                for j in range(j0, j1):
                    add_job(("sc_gu", f, j),
                            [((0, 128), wv[:, :, j * 128:(j + 1) * 128]),
                             ((128, 256), wv[:, :, DFF + j * 128:DFF + (j + 1) * 128])],
                            8, 256, 0 if f == 0 else 16, sc_gu[f][j].rearrange("p (a b) -> p a b", a=8))
                for (jq, n) in dn_loads(j0, j1):
                    add_job(("sc_dn", f, jq), [((0, D), wdv[:, jq:jq + n, :])], n, D, None,
                            sc_dn[f][jq:jq + n].rearrange("j p d -> p j d"))

        plan_ffn(0)
        wiv = w_in.rearrange("(kc p) f -> p kc f", p=128)
        for u in (2, 3, 4, 5, 0, 1):
            if u < 2:
                srcs = [((0, 256), wiv[:, :, u * 256:(u + 1) * 256])]
            else:
                i = u - 2
                srcs = [((0, 128), wiv[:, :, 512 + i * 128:512 + (i + 1) * 128]),
                        ((128, 256), wiv[:, :, 1024 + i * 128:1024 + (i + 1) * 128])]
            add_job(("sc_in", u), srcs, 8, 256, 8, sc_in[u].rearrange("p (a b) -> p a b", a=8))
        wov = w_out.rearrange("(j p) d -> p j d", p=128)
        for q in range(4):
            add_job(("sc_out", q), [((0, D), wov[:, 2 * q:2 * q + 2, :])], 2, D, None,
                    sc_out[2 * q:2 * q + 2].rearrange("j p d -> p j d"))
        plan_ffn(1)

        halfbufs = [(xs[:, 9 + h, :], [("xs", 9 + h)]) for h in range(3)]
        half_busy = [None] * 3
        stagebufs = [
            None,
            (ybuf[:, 0, :, :].rearrange("p i n -> p (i n)"), [("ybuf", 0, i) for i in range(4)]),
            (ybuf[:, 1, :, :].rearrange("p i n -> p (i n)"), [("ybuf", 1, i) for i in range(4)]),
            (lnt[:, :, :].rearrange("p r n -> p (r n)"), [("lnt", r, pi) for r in range(2) for pi in range(2)]),
            (upool[:, :, :].rearrange("p r n -> p (r n)"), [("upool", r, b_) for r in range(2) for b_ in range(9)]),
            (ptmp[:, :, :].rearrange("p r n -> p (r n)"), [("ptmp", 0), ("ptmp", 1)]),
        ]
        jit = {"on": True, "load_ptr": 0, "cast_ptr": 0, "stage": {}}
        preloaded = {}
        first_mix = job_index[("sc_in", 2)]
        first_f2 = job_index[("sc_gu", 1, 0)]
        busy = [None] * len(stagebufs)
        buf_of = {}

        def choose_buf(n):
            if first_mix + 2 <= n < first_f2 + 5:
                return 0
            return 1 + (n % (len(stagebufs) - 1))

        def jit_stage_load(n):
            jb = jobs[n]
            bi = choose_buf(n)
            if bi == 0:
                a2 = jb["a"] // 2
                pieces = []
                for h in range(2):
                    hb_i = None
                    while hb_i is None:
                        for t in range(3):
                            if half_busy[t] is None:
                                hb_i = t
                                break
                        if hb_i is None:
                            jit_cast(jit["cast_ptr"])
                            jit["cast_ptr"] += 1
                    half_busy[hb_i] = n
                    st, keys = halfbufs[hb_i]
                    v = st[:, 0:a2 * jb["b"]].rearrange("p (a b) -> p a b", a=a2)
                    for (lo, hi), src in jb["srcs"]:
                        dma("sp", v[:, :, lo:hi], src[:, h * a2:(h + 1) * a2, :], [], keys)
                    pieces.append((v, keys, h * a2, a2, hb_i))
                jit["stage"][n] = pieces
                return
            while busy[bi] is not None:
                jit_cast(jit["cast_ptr"])
                jit["cast_ptr"] += 1
            busy[bi] = n
            buf_of[n] = bi
            st, keys = stagebufs[bi]
            v = st[:, 0:jb["a"] * jb["b"]].rearrange("p (a b) -> p a b", a=jb["a"])
            for (lo, hi), src in jb["srcs"]:
                dma("sp", v[:, :, lo:hi], src, [], keys)
            jit["stage"][n] = [(v, keys, 0, jb["a"], None)]

        def jit_cast(n):
            jb = jobs[n]
            pieces = jit["stage"].pop(n)
            r = nxt("ring", NRING)
            a_, b_ = jb["a"], jb["b"]
            rv = ring[:, r, 0:a_ * b_].rearrange("p (a b) -> p a b", a=a_)
            for (v, keys, a0, an, hb_i) in pieces:
                rvp = rv[:, a0:a0 + an, :]
                if jb["gain"] is not None:
                    g0 = jb["gain"] + a0
                    gap = small[:, g0:g0 + an].unsqueeze(2).to_broadcast([128, an, b_])
                    P.op("dve", lambda e, rvp=rvp, v=v, gap=gap: e.tensor_tensor(out=rvp, in0=v, in1=gap, op=ALU.mult),
                         keys + [("small",)], [("ring", r)])
                else:
                    P.op("dve", lambda e, rvp=rvp, v=v: e.tensor_copy(out=rvp, in_=v), keys, [("ring", r)])
                if hb_i is not None:
                    half_busy[hb_i] = None
            dma("act", jb["store"], rv, [("ring", r)], [jb["key"]])
            preloaded[jb["key"]] = r
            if n in buf_of:
                busy[buf_of[n]] = None

        def jit_advance(k):
            n_jobs = len(jobs)
            while jit["load_ptr"] < min(n_jobs, k + 6):
                jit_stage_load(jit["load_ptr"])
                jit["load_ptr"] += 1
            while jit["cast_ptr"] < min(n_jobs, k + 3):
                jit_cast(jit["cast_ptr"])
                jit["cast_ptr"] += 1

        def preload_unit(src_ap, src_key):
            preloaded[src_key] = load_unit(src_ap, src_key)

        def load_unit(src_ap, src_key, view=None):
            if jit["on"]:
                jit_advance(job_index[src_key])
            if src_key in preloaded:
                return preloaded.pop(src_key)
            r = nxt("ring", NRING)
            dst = ring[:, r, :] if view is None else view(ring[:, r, :])
            dma("sp", dst, src_ap, [src_key], [("ring", r)])
            return r

        slotmap = {}
        free_slots = list(range(NXS))

        def xslot(ti, c):
            return slotmap[(ti, c)]

        def free_slot(ti, c):
            free_slots.append(slotmap[(ti, c)])

        def rms_stats(slot, rows):
            col = nxt("ssc", 16)
            P.op("act", lambda e: e.activation(out=sg[:rows, :, :].rearrange("p a b -> p (a b)"), in_=xs[:rows, slot, :],
                                               func=AF.Square, accum_out=ss[:rows, col:col + 1]),
                 [("xs", slot)], [("sg", 0), ("sg", 1), ("ss", col)])
            P.op("pool", lambda e: e.tensor_scalar(out=ss[:rows, col:col + 1], in0=ss[:rows, col:col + 1],
                                                   scalar1=1.0 / D, scalar2=EPS, op0=ALU.mult, op1=ALU.add),
                 [("ss", col)], [("ss", col)])
            P.op("pool", lambda e: e.tensor_tensor(out=rst[:rows, col:col + 1], in0=ss[:rows, col:col + 1],
                                                   in1=mhalf[:rows, :], op=ALU.pow),
                 [("ss", col), ("mhalf",)], [("rst", col)])
            return col

        def norm_scale(slot, rows, col):
            h = nxt("hn", 3)
            P.op("act", lambda e: e.activation(out=hn[:rows, h, :], in_=xs[:rows, slot, :], func=AF.Copy,
                                               scale=rst[:rows, col:col + 1]),
                 [("xs", slot), ("rst", col)], [("hn", h)])
            return h

        def norm_transpose(h, c):
            rows = 128 if c < 8 else 32
            t = nxt("tp", 2)
            tb = tbank[t]

            def fn(e):
                ins = None
                first = None
                for kc in range(8):
                    ins = e.transpose(tb[:, kc * 128:kc * 128 + rows], hn[:rows, h, kc * 128:(kc + 1) * 128],
                                      identb[:rows, :rows])
                    if first is None:
                        first = ins
                return (first, ins)
            P.op("pe", fn, [("hn", h), ("identb",)], [("tb", t)]).ninst = 8
            src = tb[:, :].rearrange("p (k t) -> p k t", k=8)[:, :, 0:rows]
            lo = c * 128
            P.op("dve", lambda e: e.tensor_copy(out=hb[:, :, lo:lo + rows], in_=src),
                 [("tb", t)], [("hb", kc, c) for kc in range(8)])

        def ffn(ti, f, nchunks, pieces, before_last_down=None, on_chunk=None, hook=None):
            for gi, (j0, j1) in enumerate(GROUPS):
                last = gi == len(GROUPS) - 1
                P.phase = "t%d.ffn%d.up%d" % (ti, f + 1, gi)
                for j in range(j0, j1):
                    r = load_unit(sc_gu[f][j], ("sc_gu", f, j))
                    wu = ring[:, r, :].rearrange("p (k c) -> p k c", k=8)
                    jj = j - j0
                    for (lo, hi) in pieces:
                        n = hi - lo
                        u = nxt("up3", 3)
                        pg, pu = pbank[2 * u], pbank[2 * u + 1]
                        hkeys = [("hb", kc, b) for kc in range(8) for b in blocks_of(lo, hi)]
                        mm_group(pg[:, :n], [(wu[:, kc, 0:128], hb[:, kc, lo:hi]) for kc in range(8)],
                                 hkeys + [("ring", r)], [("pb", 2 * u)])
                        mm_group(pu[:, :n], [(wu[:, kc, 128:256], hb[:, kc, lo:hi]) for kc in range(8)],
                                 hkeys + [("ring", r)], [("pb", 2 * u + 1)])
                        s = nxt("sg", 2)
                        P.op("act", lambda e, pg=pg, n=n, s=s: e.activation(out=sg[:, s, :n], in_=pg[:, :n], func=AF.Silu),
                             [("pb", 2 * u)], [("sg", s)])
                        P.op("dve", lambda e, pu=pu, n=n, s=s, jj=jj, lo=lo, hi=hi: e.tensor_tensor(
                            out=b16[:, jj, lo:hi], in0=sg[:, s, :n], in1=pu[:, :n], op=ALU.mult),
                            [("sg", s), ("pb", 2 * u + 1)], [("b16", jj, b) for b in blocks_of(lo, hi)])
                        if j == 0 and hook is not None:
                            hook(lo)
                    bg_step(6)
                P.phase = "t%d.ffn%d.down%d" % (ti, f + 1, gi)
                nj = j1 - j0
                slots = []
                jq = j0
                while jq < j1:
                    if jq % 2 == 0 and jq + 1 < j1:
                        r = load_unit(sc_dn[f][jq:jq + 2].rearrange("j p d -> p j d"), ("sc_dn", f, jq),
                                      view=lambda a: a.rearrange("p (j d) -> p j d", j=2))
                        slots.append((r, 0))
                        slots.append((r, 1))
                        jq += 2
                    else:
                        r = load_unit(sc_dn[f][jq], ("sc_dn", f, jq),
                                      view=lambda a: a[:, 0:D])
                        slots.append((r, 0))
                        jq += 1
                if last and before_last_down is not None:
                    before_last_down()
                for c in range(nchunks):
                    rows = 128 if c < 8 else 32
                    lo = c * 128
                    slot = xslot(ti, c)
                    for half in range(2):
                        a = (4, 5, 0, 1, 2, 3)[nxt("dacc", 6)]
                        pa = pbank[a]
                        pairs = []
                        for jj in range(nj):
                            r, o = slots[jj]
                            pairs.append((b16[:, jj, lo:lo + rows],
                                          ring[:, r, o * D + half * 512:o * D + (half + 1) * 512]))
                        mm_group(pa[:rows, :], pairs,
                                 [("b16", jj, c) for jj in range(nj)] + [("ring", r) for r, _ in slots],
                                 [("pb", a)])
                        P.op("dve", lambda e, pa=pa, rows=rows, slot=slot, half=half: e.scalar_tensor_tensor(
                            out=xs[:rows, slot, half * 512:(half + 1) * 512], in0=pa[:rows, :], scalar=0.5,
                            in1=xs[:rows, slot, half * 512:(half + 1) * 512], op0=ALU.mult, op1=ALU.add),
                            [("pb", a), ("xs", slot)], [("xs", slot)])
                    if last and on_chunk is not None:
                        on_chunk(c)

        pending_T = []

        def norm_pre(ti, c):
            rows = 128 if c < 8 else 32
            slot = xslot(ti, c)
            col = rms_stats(slot, rows)
            h = norm_scale(slot, rows, col)
            pending_T.append((h, c))

        def flush_T(keep=0):
            while len(pending_T) > keep:
                h, c = pending_T.pop(0)
                norm_transpose(h, c)

        def norm_all(ti, nchunks):
            for c in range(nchunks):
                norm_pre(ti, c)
                flush_T(keep=2)
            flush_T(0)

        def tcols(lo, hi):
            if lo < 1024:
                return [(0, hi - lo, 16 + lo)]
            return [(0, 16, 0), (16, 32, 1040)]

        def mixer(ti, hook=None):
            dma("pool", pcorr[:, ti % 2, :], pcorr_d[:, ti * 64:(ti + 1) * 64].partition_broadcast(128), [],
                [("pcorr", ti % 2)])
            P.phase = "t%d.win_vg" % ti
            for i in range(4):
                r = load_unit(sc_in[2 + i], ("sc_in", 2 + i))
                wu = ring[:, r, :].rearrange("p (k c) -> p k c", k=8)
                for (lo, hi) in PIECES_A:
                    n = hi - lo
                    u = nxt("up", 2)
                    pv, pgt = pbank[2 * u], pbank[2 * u + 1]
                    hkeys = [("hb", kc, b) for kc in range(8) for b in blocks_of(lo, hi)]
                    mm_group(pv[:, :n], [(wu[:, kc, 0:128], hb[:, kc, lo:hi]) for kc in range(8)],
                             hkeys + [("ring", r)], [("pb", 2 * u)])
                    mm_group(pgt[:, :n], [(wu[:, kc, 128:256], hb[:, kc, lo:hi]) for kc in range(8)],
                             hkeys + [("ring", r)], [("pb", 2 * u + 1)])
                    s = nxt("sg", 2)
                    P.op("act", lambda e, pgt=pgt, n=n, s=s: e.activation(out=sg[:, s, :n], in_=pgt[:, :n],
                                                                          func=AF.Tanh, scale=0.5),
                         [("pb", 2 * u + 1)], [("sg", s)])
                    for (p0, p1, t0) in tcols(lo, hi):
                        P.op("dve", lambda e, pv=pv, s=s, p0=p0, p1=p1, t0=t0, i=i: e.scalar_tensor_tensor(
                            out=b16[:, i, t0:t0 + (p1 - p0)], in0=sg[:, s, p0:p1], scalar=1.0, in1=pv[:, p0:p1],
                            op0=ALU.add, op1=ALU.mult),
                            [("sg", s), ("pb", 2 * u)], [("b16", i, b) for b in blocks_of(t0, t0 + p1 - p0)])
                    if i == 0 and hook is not None:
                        hook(lo)

            P.phase = "t%d.win_pool" % ti
            for u in range(2):
                r = load_unit(sc_in[u], ("sc_in", u))
                wu = ring[:, r, :].rearrange("p (k c) -> p k c", k=8)
                for gg in range(2):
                    g = 2 * u + gg
                    ur = g % 2
                    for (lo, hi) in PIECES_A:
                        n = hi - lo
                        a = 4 + nxt("acc", 2)
                        pa = pbank[a]
                        hkeys = [("hb", kc, b) for kc in range(8) for b in blocks_of(lo, hi)]
                        mm_group(pa[:, :n], [(wu[:, kc, gg * 128:(gg + 1) * 128], hb[:, kc, lo:hi]) for kc in range(8)],
                                 hkeys + [("ring", r)], [("pb", a)])
                        for (p0, p1, t0) in tcols(lo, hi):
                            P.op("act", lambda e, pa=pa, p0=p0, p1=p1, t0=t0, ur=ur: e.activation(
                                out=upool[:, ur, t0:t0 + (p1 - p0)], in_=pa[:, p0:p1], func=AF.Copy),
                                [("pb", a)], [("upool", ur, b) for b in blocks_of(t0, t0 + p1 - p0)])
                    U = upool[:, ur, :]
                    ukeys = [("upool", ur, b) for b in range(9)]
                    w = 2 ** (g + 1)
                    hlf = w // 2
                    cur = U
                    curkeys = ukeys
                    ext = 1056
                    step = 1
                    for lvl in range(g + 1):
                        ext2 = ext - step
                        dst = ptmp[:, lvl % 2, :]
                        P.op("dve", lambda e, cur=cur, dst=dst, ext2=ext2, step=step: e.tensor_tensor(
                            out=dst[:, 0:ext2], in0=cur[:, 0:ext2], in1=cur[:, step:step + ext2], op=ALU.add),
                            curkeys, [("ptmp", lvl % 2)])
                        cur = dst
                        curkeys = [("ptmp", lvl % 2)]
                        ext = ext2
                        step *= 2
                    sidx = g % 2
                    other = ptmp[:, (g + 1) % 2, :]
                    P.op("dve", lambda e, cur=cur, other=other, hlf=hlf, w=w: e.tensor_scalar(
                        out=other[:, 0:TT], in0=cur[:, 16 - hlf:16 - hlf + TT], scalar1=1.0 / w, scalar2=None,
                        op0=ALU.mult), curkeys, [("ptmp", (g + 1) % 2)])
                    okeys = [("ptmp", (g + 1) % 2)]
                    cb = g * 16
                    pq = ti % 2
                    P.op("dve", lambda e, other=other, cb=cb, pq=pq: e.tensor_tensor(
                        out=other[:, 0:8], in0=other[:, 0:8], in1=pcorr[:, pq, cb:cb + 8], op=ALU.mult),
                        okeys + [("pcorr", pq)], okeys)
                    P.op("dve", lambda e, other=other, cb=cb, pq=pq: e.tensor_tensor(
                        out=other[:, TT - 8:TT], in0=other[:, TT - 8:TT], in1=pcorr[:, pq, cb + 8:cb + 16], op=ALU.mult),
                        okeys + [("pcorr", pq)], okeys)
                    prow = 4 + g
                    P.op("dve", lambda e, other=other, U=U, prow=prow: e.tensor_tensor(
                        out=b16[:, prow, 0:TT], in0=other[:, 0:TT], in1=U[:, 16:16 + TT], op=ALU.subtract),
                        okeys + ukeys, [("b16", prow, b) for b in range(8)])
        pend_pool = []

        def mixer_tail(ti):
            def conv_piece(pi, after_chunk=None):
                lo, hi = PIECES_B[pi]
                pm, pe2 = pbank[2 * pi], pbank[2 * pi + 1]

                def stats(i):
                    q = i % 2
                    P.op("pe", lambda e: e.matmul(pm[:, :], onesb[:, :], ysq[:, q, 0, :], start=(i == 0), stop=(i == 3)),
                         [("ysq", q, 0), ("onesb",)], [("pb", 2 * pi)])
                    P.op("pe", lambda e: e.matmul(pe2[:, :], onesb[:, :], ysq[:, q, 1, :], start=(i == 0), stop=(i == 3)),
                         [("ysq", q, 1), ("onesb",)], [("pb", 2 * pi + 1)])

                for i in range(4):
                    a = 4 + nxt("acc", 2)
                    pa = pbank[a]
                    pairs = [(diag[:, i, k, :], b16[:, i, lo + k + 1:lo + k + 1 + 512]) for k in range(31)]
                    mm_group(pa[:, :], pairs,
                             [("b16", i, b) for b in blocks_of(lo + 1, lo + 31 + 512)] +
                             [("diag", i, k) for k in range(31)], [("pb", a)])
                    P.op("act", lambda e, pa=pa, i=i: e.activation(
                        out=ybuf[:, pi, i, :], in_=pa[:, :], func=AF.Identity, bias=small[:, 32 + i:33 + i]),
                        [("pb", a), ("small",)], [("ybuf", pi, i)])
                    q = i % 2
                    P.op("act", lambda e, pa=pa, i=i, q=q: e.activation(out=ysq[:, q, 0, :], in_=pa[:, :], func=AF.Identity,
                                                                        bias=small[:, 32 + i:33 + i]),
                         [("pb", a), ("small",)], [("ysq", q, 0)])
                    P.op("act", lambda e, pa=pa, i=i, q=q: e.activation(out=ysq[:, q, 1, :], in_=pa[:, :], func=AF.Square,
                                                                        bias=small[:, 32 + i:33 + i]),
                         [("pb", a), ("small",)], [("ysq", q, 1)])
                    if i >= 1:
                        stats(i - 1)
                    if after_chunk is not None:
                        after_chunk(i)
                stats(3)

            def ln_head(pi, part):
                pm, pe2 = pbank[2 * pi], pbank[2 * pi + 1]
                sl = slice(pi * 512, (pi + 1) * 512)
                if part >= 1:
                    if part == 1:
                        P.op("act", lambda e: e.activation(out=lnt[:, 1, sl], in_=lnt[:, 1, sl], func=AF.Sqrt),
                             [("lnt", 1, pi)], [("lnt", 1, pi)])
                    hs = slice(pi * 512 + (part - 1) * 256, pi * 512 + part * 256)
                    P.op("dve", lambda e: e.reciprocal(out=lnt[:, 1, hs], in_=lnt[:, 1, hs]),
                         [("lnt", 1, pi)], [("lnt", 1, pi)])
                    return
                P.op("act", lambda e: e.activation(out=lnt[:, 0, sl], in_=pm[:, :], func=AF.Copy),
                     [("pb", 2 * pi)], [("lnt", 0, pi)])
                P.op("act", lambda e: e.activation(out=lnt[:, 1, sl], in_=pm[:, :], func=AF.Square),
                     [("pb", 2 * pi)], [("lnt", 1, pi)])
                P.op("dve", lambda e: e.scalar_tensor_tensor(out=lnt[:, 1, sl], in0=pe2[:, :], scalar=EPS, in1=lnt[:, 1, sl],
                                                             op0=ALU.add, op1=ALU.subtract),
                     [("pb", 2 * pi + 1), ("lnt", 1, pi)], [("lnt", 1, pi)])
                P.op("dve", lambda e: e.tensor_scalar(out=lnt[:, 1, sl], in0=lnt[:, 1, sl], scalar1=1e-12,
                                                      scalar2=None, op0=ALU.max),
                     [("lnt", 1, pi)], [("lnt", 1, pi)])

            def ln_chunk(pi, i):
                lo, hi = PIECES_B[pi]
                sl = slice(pi * 512, (pi + 1) * 512)
                P.op("pool", lambda e: e.tensor_tensor(out=ybuf[:, pi, i, :], in0=ybuf[:, pi, i, :], in1=lnt[:, 0, sl],
                                                       op=ALU.subtract), [("ybuf", pi, i), ("lnt", 0, pi)], [("ybuf", pi, i)])
                P.op("pool", lambda e: e.tensor_tensor(out=ybuf[:, pi, i, :], in0=ybuf[:, pi, i, :], in1=lnt[:, 1, sl],
                                                       op=ALU.mult), [("ybuf", pi, i), ("lnt", 1, pi)], [("ybuf", pi, i)])
                P.op("act", lambda e: e.activation(out=hb[:, 4 + i, lo:hi], in_=ybuf[:, pi, i, :], func=AF.Silu,
                                                   scale=small[:, 36 + i:37 + i], bias=small[:, 40 + i:41 + i]),
                     [("ybuf", pi, i), ("small",)], [("hb", 4 + i, b) for b in blocks_of(lo, hi)])

            P.phase = "t%d.convA" % ti
            conv_piece(0)
            P.phase = "t%d.poolw" % ti
            for g in range(4):
                for (lo, hi) in PIECES_B:
                    a = 4 + nxt("acc", 2)
                    pa = pbank[a]
                    prow = 4 + g
                    mm_group(pa[:, :], [(poolw[:, g, :], b16[:, prow, lo:hi])],
                             [("b16", prow, b) for b in blocks_of(lo, hi)] + [("poolw",)], [("pb", a)])
                    P.op("act", lambda e, pa=pa, g=g, lo=lo, hi=hi: e.activation(
                        out=hb[:, g, lo:hi], in_=pa[:, :], func=AF.Identity,
                        scale=small[:, 28 + g:29 + g], bias=pbias[:, g:g + 1]),
                        [("pb", a), ("small",), ("pbias",)], [("hb", g, b) for b in blocks_of(lo, hi)])
            P.phase = "t%d.convB" % ti

            def lnA(i):
                if i == 0:
                    ln_head(0, 0)
                elif i == 1:
                    ln_head(0, 1)
                    ln_head(0, 2)
                elif i == 2:
                    ln_chunk(0, 0)
                    ln_chunk(0, 1)
                elif i == 3:
                    ln_chunk(0, 2)
                    ln_chunk(0, 3)
            conv_piece(1, after_chunk=lnA)
            P.phase = "t%d.wout" % ti
            slots = []
            for q in range(4):
                r = load_unit(sc_out[2 * q:2 * q + 2].rearrange("j p d -> p j d"), ("sc_out", q),
                              view=lambda a: a.rearrange("p (j d) -> p j d", j=2))
                slots += [(r, 0), (r, 1)]

            def wout_part(c, cis):
                lo = c * 128
                slot = xslot(ti, c)
                for half in range(2):
                    a = 4 + nxt("acc", 2)
                    pa = pbank[a]
                    pairs = [(hb[:, ci, lo:lo + 128],
                              ring[:, slots[ci][0], slots[ci][1] * D + half * 512:slots[ci][1] * D + (half + 1) * 512])
                             for ci in cis]
                    mm_group(pa[:, :], pairs, [("hb", ci, c) for ci in cis] + [("ring", slots[ci][0]) for ci in cis],
                             [("pb", a)])
                    P.op("dve", lambda e, pa=pa, slot=slot, half=half: e.tensor_tensor(
                        out=xs[:, slot, half * 512:(half + 1) * 512], in0=pa[:, :],
                        in1=xs[:, slot, half * 512:(half + 1) * 512], op=ALU.add),
                        [("pb", a), ("xs", slot)], [("xs", slot)])

            for c in range(8):
                wout_part(c, range(0, 4))
                if c == 0:
                    ln_head(1, 0)
                elif c == 2:
                    ln_head(1, 1)
                elif c == 3:
                    ln_head(1, 2)
                elif c >= 4:
                    ln_chunk(1, c - 4)
            P.phase = "t%d.wout2" % ti
            for c in range(8):
                wout_part(c, range(4, 8))
                norm_pre(ti, c)
                flush_T(keep=2)

        def load_chunk(ti, c):
            slot = free_slots.pop(0)
            slotmap[(ti, c)] = slot
            if c < 8:
                dma("pool", xs[:, slot, :], xt[ti, c * 128:(c + 1) * 128, :], [], [("xs", slot)])
            else:
                dma("pool", xs[:32, slot, :], xt[ti, TT:TT + 32, :], [], [("xs", slot)])

        nxt_state = {"to_load": [], "to_norm": []}

        def load_more(ti):
            while free_slots and nxt_state["to_load"]:
                c = nxt_state["to_load"].pop(0)
                load_chunk(ti, c)
                nxt_state["to_norm"].append(c)

        def norm_more(ti):
            while nxt_state["to_norm"] and len(pending_T) < 3:
                norm_pre(ti, nxt_state["to_norm"].pop(0))

        fin_pending = []

        def final_stats(ti, c):
            slot = xslot(ti, c)
            col = rms_stats(slot, 128)
            fin_pending.append((ti, c, slot, col))

        def final_finish():
            ti, c, slot, col = fin_pending.pop(0)
            P.op("dve", lambda e: e.scalar_tensor_tensor(
                out=xs[:, slot, :], in0=xs[:, slot, :], scalar=rst[:, col:col + 1], in1=gfb[:, :],
                op0=ALU.mult, op1=ALU.mult),
                [("xs", slot), ("rst", col), ("gfb",)], [("xs", slot)])
            dma("sp", yt[ti, c * 128:(c + 1) * 128, :], xs[:, slot, :], [("xs", slot)], [("yt", ti, c)])
            free_slot(ti, c)

        def flush_all(lo):
            if lo == 0:
                flush_T(1 if (pending_T and pending_T[-1][1] == 8) else 0)
            else:
                flush_T(0)

        jit_advance(0)
        for c in range(NCH):
            load_chunk(0, c)
        norm_all(0, NCH)
        for ti in range(ntiles):
            has_next = ti + 1 < ntiles

            def mix_cb(c, ti=ti):
                norm_pre(ti, c)
                if c == 8:
                    free_slot(ti, 8)
                flush_T(keep=2)

            ffn(ti, 0, NCH, PIECES_A, on_chunk=mix_cb, hook=flush_all)
            mixer(ti, hook=flush_all)
            flush_T(0)
            mixer_tail(ti)

            def pre_last(ti=ti, has_next=has_next):
                if has_next:
                    nxt_state["to_load"] = list(range(NCH))
                    nxt_state["to_norm"] = []
                    load_more(ti + 1)
                    norm_more(ti + 1)
                    jit["on"] = False
                    for j in range(4):
                        preload_unit(sc_gu[0][j], ("sc_gu", 0, j))

            def fin_cb(c, ti=ti, has_next=has_next):
                final_stats(ti, c)
                if c >= 1:
                    final_finish()
                if has_next:
                    norm_more(ti + 1)
                    flush_T(keep=2)
                    load_more(ti + 1)

            ffn(ti, 1, 8, PIECES_B, before_last_down=pre_last, on_chunk=fin_cb, hook=flush_all)
            if has_next:
                load_more(ti + 1)
                while nxt_state["to_norm"] or len(pending_T) > 2:
                    norm_more(ti + 1)
                    flush_T(keep=2)
                flush_T(1 if (pending_T and pending_T[-1][1] == 8) else 0)
            final_finish()
        flush_T(0)

        with nc.Block() as block:
            fw = [((q, i), P.dma_count[q][i] * 16) for q in NDSEM for i in range(NDSEM[q]) if P.dma_count[q][i] > 0]
            P.emit(nc, block, esem, dsem, fw)
    return nc


def _core_inputs(x_prompt, x_sample, core, ntiles=12):
    segs = []
    for s in range(2):
        segs.append((x_sample[2 * core + s], 0, 4096))
    b, q = core // 4, core % 4
    segs.append((x_prompt[b], q * 4096, (q + 1) * 4096))
    xt = np.zeros((ntiles, NCOL, D), np.float32)
    pc = np.ones((ntiles, 4, 16), np.float32)
    ti = 0
    for (seq, s0, s1) in segs:
        L = seq.shape[0]
        for i in range((s1 - s0) // TT):
            if ti >= ntiles:
                break
            t0 = s0 + i * TT
            xt[ti, :TT] = seq[t0:t0 + TT]
            if t0 - HALO >= 0:
                xt[ti, TT:TT + HALO] = seq[t0 - HALO:t0]
            if t0 + TT + HALO <= L:
                xt[ti, TT + HALO:] = seq[t0 + TT:t0 + TT + HALO]
            for g in range(4):
                w = 2 ** (g + 1)
                h = w // 2
                for m in range(8):
                    t = t0 + m
                    c = min(t + h, L) - max(t - h, 0)
                    pc[ti, g, m] = w / c
                    t = t0 + TT - 8 + m
                    c = min(t + h, L) - max(t - h, 0)
                    pc[ti, g, 8 + m] = w / c
            ti += 1
    return xt, pc.reshape(1, -1)


def _small(inp):
    sm = np.zeros((128, NSMALL), np.float32)
    sm[:, 0:8] = inp["ffn1_norm"][0].reshape(8, 128).T
    sm[:, 8:16] = inp["mix_norm"][0].reshape(8, 128).T
    sm[:, 16:24] = inp["ffn2_norm"][0].reshape(8, 128).T
    sm[:, 24:28] = inp["pool_b"][0].T
    sm[:, 28:32] = inp["pool_scale"][0].reshape(4, 128).T
    sm[:, 32:36] = inp["dw_b"][0].reshape(4, 128).T
    sm[:, 36:40] = inp["conv_ln_g"][0].reshape(4, 128).T
    sm[:, 40:44] = inp["conv_ln_b"][0].reshape(4, 128).T
    dw = inp["dw_w"][0]
    sm[:, 44:168] = dw.reshape(31, 4, 128).transpose(2, 1, 0).reshape(128, 124)
    return sm


_NC_CACHE = {}


def _shared_maps(inp):
    f = lambda a: np.ascontiguousarray(np.asarray(a, dtype=np.float32))
    return {
        "w_gu1": f(inp["ffn1_w_gu"][0]), "w_gu2": f(inp["ffn2_w_gu"][0]),
        "w_dn1": f(inp["ffn1_w_down"][0]), "w_dn2": f(inp["ffn2_w_down"][0]),
        "w_in": f(inp["w_in"][0]), "w_out": f(inp["w_out"][0]),
        "pool_w": f(inp["pool_w"][0].reshape(512, 128)),
        "small": _small(inp), "gfin": f(inp["final_norm"].reshape(1, D)),
        "ident": np.eye(128, dtype=np.float32),
    }


def kernel(**inputs):
    inp = {k: np.asarray(v) for k, v in inputs.items()}
    xp, xsm = inp["x_prompt"], inp["x_sample"]
    ntiles = 12
    if ntiles not in _NC_CACHE:
        _NC_CACHE[ntiles] = build_nc(ntiles)
    nc = _NC_CACHE[ntiles]
    shared = _shared_maps(inp)
    in_maps = []
    for core in range(8):
        xt, pc = _core_inputs(xp, xsm, core)
        m = dict(shared)
        m["xt"] = xt
        m["pcorr"] = pc
        in_maps.append(m)
    res = run_bass_kernel_spmd(nc, in_maps, core_ids=list(range(8)))
    y_prompt = np.empty_like(xp, dtype=np.float32)
    y_sample = np.empty_like(xsm, dtype=np.float32)
    for core in range(8):
        yt = np.asarray(res.results[core]["yt"]).reshape(12, TT, D)
        y_sample[2 * core] = yt[0:4].reshape(4096, D)
        y_sample[2 * core + 1] = yt[4:8].reshape(4096, D)
        b, q = core // 4, core % 4
        y_prompt[b, q * 4096:(q + 1) * 4096] = yt[8:12].reshape(4096, D)
    return (y_prompt, y_sample)
```

```python
import numpy as np
import concourse.bass as bass
import concourse.mybir as mybir
from concourse.bass_utils import run_bass_kernel_spmd

F32 = mybir.dt.float32
BF16 = mybir.dt.bfloat16
AF = mybir.ActivationFunctionType
ALU = mybir.AluOpType

D = 1024
DFF = 2816
NFC = 22
TT = 1024
HALO = 16
NCOL = TT + 2 * HALO
NCH = 9
NXS = 12
NRING = 8
EPS = 1e-6
GROUPS = [(0, 7), (7, 14), (14, 22)]
PIECES_A = [(0, 512), (512, 1024), (1024, 1056)]
PIECES_B = [(0, 512), (512, 1024)]
NSMALL = 168
ENGS = ["pe", "act", "dve", "pool", "sp"]
NDSEM = {"sp": 12, "pool": 12, "act": 10}
ATTACH_WAITS = True


def blocks_of(lo, hi):
    return range(lo // 128, (hi - 1) // 128 + 1)


class Op:
    __slots__ = ("eng", "fn", "deps", "ddeps", "signal", "idx", "dma", "clock", "label", "ninst")


class Prog:
    def __init__(self):
        self.eng_ops = {e: [] for e in ENGS}
        self.last_writer = {}
        self.readers = {}
        self.known = {e: {} for e in ENGS}
        self.dma_count = {q: [0] * n for q, n in NDSEM.items()}
        self.dma_rr = {q: 0 for q in NDSEM}
        self.phase = ""

    def _collect(self, eng, reads, writes):
        toks = []
        for k in reads:
            w = self.last_writer.get(k)
            if w is not None:
                toks.append(w)
        for k in writes:
            w = self.last_writer.get(k)
            if w is not None:
                toks.append(w)
            toks.extend(self.readers.get(k, ()))
        need = {}
        for t in toks:
            if t[0] == "e" and t[1] == "pe" and eng == "pe":
                continue
            key = (t[0], t[1])
            if need.get(key, -1) < t[2]:
                need[key] = t[2]
        kn = self.known[eng]
        out = {}
        for key, v in need.items():
            if kn.get(key, -1) < v:
                out[key] = v
        return out

    def _learn(self, eng, need):
        kn = self.known[eng]
        for key, v in need.items():
            if kn.get(key, -1) < v:
                kn[key] = v
            if key[0] == "e":
                clk = self.eng_ops[key[1]][v].clock
                for k2, v2 in clk.items():
                    if kn.get(k2, -1) < v2:
                        kn[k2] = v2

    def op(self, eng, fn, reads=(), writes=(), dma=False):
        need = self._collect(eng, reads, writes)
        if dma:
            si0 = self.dma_rr[eng]
            prev = self.dma_count[eng][si0] * 16
            if prev > 0 and self.known[eng].get(("d", (eng, si0)), -1) < prev:
                need[("d", (eng, si0))] = max(need.get(("d", (eng, si0)), -1), prev)
        if len(need) > 1:
            kn = dict(self.known[eng])
            items = list(need.items())
            chosen = {}
            while items:
                best = None
                for idx, (key, v) in enumerate(items):
                    clk = self.eng_ops[key[1]][v].clock if key[0] == "e" else {}
                    cov = 0
                    for (k2, v2) in items:
                        if k2 != key and max(kn.get(k2, -1), clk.get(k2, -1)) >= v2:
                            cov += 1
                    if best is None or cov > best[0]:
                        best = (cov, idx)
                key, v = items.pop(best[1])
                chosen[key] = v
                if kn.get(key, -1) < v:
                    kn[key] = v
                if key[0] == "e":
                    for k2, v2 in self.eng_ops[key[1]][v].clock.items():
                        if kn.get(k2, -1) < v2:
                            kn[k2] = v2
                items = [(k2, v2) for (k2, v2) in items if kn.get(k2, -1) < v2]
            need = chosen
        o = Op()
        o.eng = eng
        o.fn = fn
        o.deps = need
        o.signal = False
        o.dma = None
        o.idx = len(self.eng_ops[eng])
        o.label = self.phase
        o.ninst = 1
        for key, v in need.items():
            if key[0] == "e":
                self.eng_ops[key[1]][v].signal = True
        self._learn(eng, need)
        o.clock = dict(self.known[eng])
        self.eng_ops[eng].append(o)
        if dma:
            q = eng
            si = self.dma_rr[q]
            self.dma_rr[q] = (si + 1) % NDSEM[q]
            self.dma_count[q][si] += 1
            tok = ("d", (q, si), self.dma_count[q][si] * 16)
            o.dma = (q, si)
        else:
            tok = ("e", eng, o.idx)
        for k in writes:
            self.last_writer[k] = tok
            self.readers[k] = []
        for k in reads:
            self.readers.setdefault(k, []).append(tok)
        return o

    def emit(self, nc, block, esem, dsem, final_waits):
        for e in ENGS:
            ms = 0
            for o in self.eng_ops[e]:
                if o.signal:
                    ms += 1
                    o.signal = ms
        handles = {"pe": block.tensor, "act": block.scalar, "dve": block.vector,
                   "pool": block.gpsimd, "sp": block.sync}
        prog = self

        def make(ename):
            def body(eng):
                for o in prog.eng_ops[ename]:
                    deps = []
                    for key, v in o.deps.items():
                        if key[0] == "e":
                            deps.append((esem[key[1]], prog.eng_ops[key[1]][v].signal))
                        else:
                            deps.append((dsem[key[1]], v))
                    attach = None
                    if deps and ATTACH_WAITS and o.dma is None and ename in ("pe", "act", "dve", "pool"):
                        attach = deps.pop()
                    for sem, v in deps:
                        eng.wait_ge(sem, v)
                    ins = o.fn(eng)
                    first = ins
                    if isinstance(ins, tuple):
                        first, ins = ins
                    if attach is not None:
                        first._wait_ge(attach[0], attach[1])
                    if o.dma is not None:
                        ins.then_inc(dsem[o.dma], 16)
                    elif o.signal:
                        ins.then_inc(esem[ename], 1)
                if ename == "sp":
                    for key, v in final_waits:
                        eng.wait_ge(dsem[key], v)
            return body

        for e in ENGS:
            handles[e](make(e))


def build_nc(ntiles):
    nc = bass.Bass("TRN2", target_bir_lowering=False)
    P = Prog()

    def dram_in(name, shape, dt=F32):
        return nc.dram_tensor(name, list(shape), dt, kind="ExternalInput").ap()

    xt = dram_in("xt", [ntiles, NCOL, D])
    w_gu = [dram_in("w_gu1", [D, 2 * DFF]), dram_in("w_gu2", [D, 2 * DFF])]
    w_dn = [dram_in("w_dn1", [DFF, D]), dram_in("w_dn2", [DFF, D])]
    w_in = dram_in("w_in", [D, 1536])
    w_out = dram_in("w_out", [D, D])
    pool_w = dram_in("pool_w", [512, 128])
    small_d = dram_in("small", [128, NSMALL])
    gfin_d = dram_in("gfin", [1, D])
    pcorr_d = dram_in("pcorr", [1, ntiles * 64])
    ident_d = dram_in("ident", [128, 128])
    yt = nc.dram_tensor("yt", [ntiles, TT, D], F32, kind="ExternalOutput").ap()

    sc_gu = [nc.dram_tensor("sc_gu%d" % i, [NFC, 128, 2048], BF16).ap() for i in (1, 2)]
    sc_dn = [nc.dram_tensor("sc_dn%d" % i, [NFC, 128, D], BF16).ap() for i in (1, 2)]
    sc_in = nc.dram_tensor("sc_in", [6, 128, 2048], BF16).ap()
    sc_out = nc.dram_tensor("sc_out", [8, 128, D], BF16).ap()

    import contextlib
    es = contextlib.ExitStack()
    with es:
        def sb(name, shape, dt):
            return es.enter_context(nc.sbuf_tensor(name, list(shape), dt))

        def ps(name, shape, dt):
            return es.enter_context(nc.psum_tensor(name, list(shape), dt))

        xs = sb("xs", [128, NXS, D], F32)
        hn = sb("hn", [128, 3, D], BF16)
        hb = sb("hb", [128, 8, NCOL], BF16)
        b16 = sb("b16", [128, 8, NCOL], BF16)
        sg = sb("sg", [128, 2, 512], F32)
        upool = sb("upool", [128, 2, NCOL], F32)
        ptmp = sb("ptmp", [128, 2, NCOL], F32)
        ybuf = sb("ybuf", [128, 2, 4, 512], F32)
        ysq = sb("ysq", [128, 2, 2, 512], BF16)
        lnt = sb("lnt", [128, 2, 1024], F32)
        diag = sb("diag", [128, 4, 31, 128], BF16)
        ring = sb("ring", [128, NRING, 2048], BF16)
        gfb = sb("gfb_sb", [128, D], F32)
        small = sb("small_sb", [128, NSMALL], F32)
        whalf = sb("whalf", [128, 124], F32)
        pbias = sb("pbias", [128, 4], F32)
        identf = sb("identf", [128, 128], F32)
        identb = sb("identb", [128, 128], BF16)
        onesm = sb("onesm", [128, 128], F32)
        onesb = sb("onesb", [128, 128], BF16)
        mhalf = sb("mhalf", [128, 1], F32)
        poolw = sb("poolw", [128, 4, 128], BF16)
        pcorr = sb("pcorr_sb", [128, 2, 64], F32)
        ss = sb("ss", [128, 16], F32)
        rst = sb("rst", [128, 16], F32)

        pbank = [ps("pb%d" % i, [128, 512], F32) for i in range(6)]
        tbank = [ps("tb%d" % i, [128, 1024], BF16) for i in range(2)]

        esem = {e: es.enter_context(nc.semaphore("e_" + e)) for e in ENGS}
        dsem = {}
        for q, n in NDSEM.items():
            for i in range(n):
                dsem[(q, i)] = es.enter_context(nc.semaphore("d_%s%d" % (q, i)))

        cnt = {"up": 0, "acc": 0, "tp": 0, "ring": 0, "ssc": 0, "hn": 0, "sg": 0, "ost": 0, "nsc": 0, "dacc": 0, "up3": 0}

        def nxt(name, mod):
            v = cnt[name]
            cnt[name] = v + 1
            return v % mod

        def dma(q, out_ap, in_ap, reads, writes):
            return P.op(q, lambda e, o=out_ap, i=in_ap: e.dma_start(out=o, in_=i),
                        reads=reads, writes=writes, dma=True)

        def mm_group(out_ap, pairs, reads, writes):
            def fn(e, out_ap=out_ap, pairs=pairs):
                n = len(pairs)
                ins = None
                first = None
                for i, (l, r) in enumerate(pairs):
                    ins = e.matmul(out_ap, l, r, start=(i == 0), stop=(i == n - 1))
                    if first is None:
                        first = ins
                return (first, ins)
            o = P.op("pe", fn, reads=reads, writes=writes)
            o.ninst = len(pairs)
            return o

        dma("sp", small[:], small_d, [], [("small",)])
        dma("sp", identf[:], ident_d, [], [("identf",)])
        dma("sp", gfb[:], gfin_d.partition_broadcast(128), [], [("gfb",)])
        P.op("dve", lambda e: e.tensor_copy(out=identb[:], in_=identf[:]), [("identf",)], [("identb",)])
        P.op("dve", lambda e: e.memset(onesm[:], 1.0 / 512.0), [], [("onesm",)])
        P.op("dve", lambda e: e.memset(onesb[:], 1.0 / 512.0), [], [("onesb",)])
        P.op("pool", lambda e: e.memset(mhalf[:], -0.5), [], [("mhalf",)])
        P.op("dve", lambda e: e.tensor_scalar(out=whalf[:], in0=small[:, 44:168], scalar1=0.5, scalar2=None,
                                              op0=ALU.mult), [("small",)], [("whalf",)])
        P.op("dve", lambda e: e.tensor_tensor(out=pbias[:], in0=small[:, 24:28], in1=small[:, 28:32],
                                              op=ALU.mult), [("small",)], [("pbias",)])
        bg_work = []
        for i in range(4):
            for k in range(31):
                bg_work.append((i, k))

        def bg_step(n):
            for _ in range(n):
                if not bg_work:
                    return
                i, k = bg_work.pop(0)
                P.op("dve", lambda e, i=i, k=k: e.tensor_scalar(
                    out=diag[:, i, k, :], in0=identf[:], scalar1=whalf[:, i * 31 + k:i * 31 + k + 1],
                    scalar2=None, op0=ALU.mult),
                    [("identf",), ("whalf",)], [("diag", i, k)])

        pwst = xs[:, 0, 0:512].rearrange("p (g d) -> p g d", g=4)
        dma("sp", pwst, pool_w.rearrange("(g c) d -> c g d", g=4), [], [("xs", 0)])
        P.op("dve", lambda e: e.tensor_copy(out=poolw[:], in_=pwst), [("xs", 0)], [("poolw",)])

        jobs = []
        job_index = {}

        def add_job(key, srcs, a_, b_, gain, store_ap):
            job_index[key] = len(jobs)
            jobs.append(dict(key=key, srcs=srcs, a=a_, b=b_, gain=gain, store=store_ap))

        def dn_loads(j0, j1):
            out = []
            jq = j0
            while jq < j1:
                if jq % 2 == 0 and jq + 1 < j1:
                    out.append((jq, 2))
                    jq += 2
                else:
                    out.append((jq, 1))
                    jq += 1
            return out

        def plan_ffn(f):
            wv = w_gu[f].rearrange("(kc p) f -> p kc f", p=128)
            wdv = w_dn[f].rearrange("(j p) d -> p j d", p=128)
            for (j0, j1) in GROUPS:
                for j in range(j0, j1):
                    add_job(("sc_gu", f, j),
                            [((0, 128), wv[:, :, j * 128:(j + 1) * 128]),
                             ((128, 256), wv[:, :, DFF + j * 128:DFF + (j + 1) * 128])],
                            8, 256, 0 if f == 0 else 16, sc_gu[f][j].rearrange("p (a b) -> p a b", a=8))
                for (jq, n) in dn_loads(j0, j1):
                    add_job(("sc_dn", f, jq), [((0, D), wdv[:, jq:jq + n, :])], n, D, None,
                            sc_dn[f][jq:jq + n].rearrange("j p d -> p j d"))

        plan_ffn(0)
        wiv = w_in.rearrange("(kc p) f -> p kc f", p=128)
        for u in (2, 3, 4, 5, 0, 1):
            if u < 2:
                srcs = [((0, 256), wiv[:, :, u * 256:(u + 1) * 256])]
            else:
                i = u - 2
                srcs = [((0, 128), wiv[:, :, 512 + i * 128:512 + (i + 1) * 128]),
                        ((128, 256), wiv[:, :, 1024 + i * 128:1024 + (i + 1) * 128])]
            add_job(("sc_in", u), srcs, 8, 256, 8, sc_in[u].rearrange("p (a b) -> p a b", a=8))
        wov = w_out.rearrange("(j p) d -> p j d", p=128)
        for q in range(4):
            add_job(("sc_out", q), [((0, D), wov[:, 2 * q:2 * q + 2, :])], 2, D, None,
                    sc_out[2 * q:2 * q + 2].rearrange("j p d -> p j d"))
        plan_ffn(1)

        halfbufs = [(xs[:, 9 + h, :], [("xs", 9 + h)]) for h in range(3)]
        half_busy = [None] * 3
        stagebufs = [
            None,
            (ybuf[:, 0, :, :].rearrange("p i n -> p (i n)"), [("ybuf", 0, i) for i in range(4)]),
            (ybuf[:, 1, :, :].rearrange("p i n -> p (i n)"), [("ybuf", 1, i) for i in range(4)]),
            (lnt[:, :, :].rearrange("p r n -> p (r n)"), [("lnt", r, pi) for r in range(2) for pi in range(2)]),
            (upool[:, :, :].rearrange("p r n -> p (r n)"), [("upool", r, b_) for r in range(2) for b_ in range(9)]),
            (ptmp[:, :, :].rearrange("p r n -> p (r n)"), [("ptmp", 0), ("ptmp", 1)]),
        ]
        jit = {"on": True, "load_ptr": 0, "cast_ptr": 0, "stage": {}}
        preloaded = {}
        first_mix = job_index[("sc_in", 2)]
        first_f2 = job_index[("sc_gu", 1, 0)]
        busy = [None] * len(stagebufs)
        buf_of = {}

        def choose_buf(n):
            if first_mix + 2 <= n < first_f2 + 5:
                return 0
            return 1 + (n % (len(stagebufs) - 1))

        def jit_stage_load(n):
            jb = jobs[n]
            bi = choose_buf(n)
            if bi == 0:
                a2 = jb["a"] // 2
                pieces = []
                for h in range(2):
                    hb_i = None
                    while hb_i is None:
                        for t in range(3):
                            if half_busy[t] is None:
                                hb_i = t
                                break
                        if hb_i is None:
                            jit_cast(jit["cast_ptr"])
                            jit["cast_ptr"] += 1
                    half_busy[hb_i] = n
                    st, keys = halfbufs[hb_i]
                    v = st[:, 0:a2 * jb["b"]].rearrange("p (a b) -> p a b", a=a2)
                    for (lo, hi), src in jb["srcs"]:
                        dma("sp", v[:, :, lo:hi], src[:, h * a2:(h + 1) * a2, :], [], keys)
                    pieces.append((v, keys, h * a2, a2, hb_i))
                jit["stage"][n] = pieces
                return
            while busy[bi] is not None:
                jit_cast(jit["cast_ptr"])
                jit["cast_ptr"] += 1
            busy[bi] = n
            buf_of[n] = bi
            st, keys = stagebufs[bi]
            v = st[:, 0:jb["a"] * jb["b"]].rearrange("p (a b) -> p a b", a=jb["a"])
            for (lo, hi), src in jb["srcs"]:
                dma("sp", v[:, :, lo:hi], src, [], keys)
            jit["stage"][n] = [(v, keys, 0, jb["a"], None)]

        def jit_cast(n):
            jb = jobs[n]
            pieces = jit["stage"].pop(n)
            r = nxt("ring", NRING)
            a_, b_ = jb["a"], jb["b"]
            rv = ring[:, r, 0:a_ * b_].rearrange("p (a b) -> p a b", a=a_)
            for (v, keys, a0, an, hb_i) in pieces:
                rvp = rv[:, a0:a0 + an, :]
                if jb["gain"] is not None:
                    g0 = jb["gain"] + a0
                    gap = small[:, g0:g0 + an].unsqueeze(2).to_broadcast([128, an, b_])
                    P.op("dve", lambda e, rvp=rvp, v=v, gap=gap: e.tensor_tensor(out=rvp, in0=v, in1=gap, op=ALU.mult),
                         keys + [("small",)], [("ring", r)])
                else:
                    P.op("dve", lambda e, rvp=rvp, v=v: e.tensor_copy(out=rvp, in_=v), keys, [("ring", r)])
                if hb_i is not None:
                    half_busy[hb_i] = None
            dma("act", jb["store"], rv, [("ring", r)], [jb["key"]])
            preloaded[jb["key"]] = r
            if n in buf_of:
                busy[buf_of[n]] = None

        def jit_advance(k):
            n_jobs = len(jobs)
            while jit["load_ptr"] < min(n_jobs, k + 6):
                jit_stage_load(jit["load_ptr"])
                jit["load_ptr"] += 1
            while jit["cast_ptr"] < min(n_jobs, k + 3):
                jit_cast(jit["cast_ptr"])
                jit["cast_ptr"] += 1

        def preload_unit(src_ap, src_key):
            preloaded[src_key] = load_unit(src_ap, src_key)

        def load_unit(src_ap, src_key, view=None):
            if jit["on"]:
                jit_advance(job_index[src_key])
            if src_key in preloaded:
                return preloaded.pop(src_key)
            r = nxt("ring", NRING)
            dst = ring[:, r, :] if view is None else view(ring[:, r, :])
            dma("sp", dst, src_ap, [src_key], [("ring", r)])
            return r

        slotmap = {}
        free_slots = list(range(NXS))

        def xslot(ti, c):
            return slotmap[(ti, c)]

        def free_slot(ti, c):
            free_slots.append(slotmap[(ti, c)])

        def rms_stats(slot, rows):
            col = nxt("ssc", 16)
            P.op("act", lambda e: e.activation(out=sg[:rows, :, :].rearrange("p a b -> p (a b)"), in_=xs[:rows, slot, :],
                                               func=AF.Square, accum_out=ss[:rows, col:col + 1]),
                 [("xs", slot)], [("sg", 0), ("sg", 1), ("ss", col)])
            P.op("pool", lambda e: e.tensor_scalar(out=ss[:rows, col:col + 1], in0=ss[:rows, col:col + 1],
                                                   scalar1=1.0 / D, scalar2=EPS, op0=ALU.mult, op1=ALU.add),
                 [("ss", col)], [("ss", col)])
            P.op("pool", lambda e: e.tensor_tensor(out=rst[:rows, col:col + 1], in0=ss[:rows, col:col + 1],
                                                   in1=mhalf[:rows, :], op=ALU.pow),
                 [("ss", col), ("mhalf",)], [("rst", col)])
            return col

        def norm_scale(slot, rows, col):
            h = nxt("hn", 3)
            P.op("act", lambda e: e.activation(out=hn[:rows, h, :], in_=xs[:rows, slot, :], func=AF.Copy,
                                               scale=rst[:rows, col:col + 1]),
                 [("xs", slot), ("rst", col)], [("hn", h)])
            return h

        def norm_transpose(h, c):
            rows = 128 if c < 8 else 32
            t = nxt("tp", 2)
            tb = tbank[t]

            def fn(e):
                ins = None
                first = None
                for kc in range(8):
                    ins = e.transpose(tb[:, kc * 128:kc * 128 + rows], hn[:rows, h, kc * 128:(kc + 1) * 128],
                                      identb[:rows, :rows])
                    if first is None:
                        first = ins
                return (first, ins)
            P.op("pe", fn, [("hn", h), ("identb",)], [("tb", t)]).ninst = 8
            src = tb[:, :].rearrange("p (k t) -> p k t", k=8)[:, :, 0:rows]
            lo = c * 128
            P.op("dve", lambda e: e.tensor_copy(out=hb[:, :, lo:lo + rows], in_=src),
                 [("tb", t)], [("hb", kc, c) for kc in range(8)])

        def ffn(ti, f, nchunks, pieces, before_last_down=None, on_chunk=None, hook=None):
            for gi, (j0, j1) in enumerate(GROUPS):
                last = gi == len(GROUPS) - 1
                P.phase = "t%d.ffn%d.up%d" % (ti, f + 1, gi)
                for j in range(j0, j1):
                    r = load_unit(sc_gu[f][j], ("sc_gu", f, j))
                    wu = ring[:, r, :].rearrange("p (k c) -> p k c", k=8)
                    jj = j - j0
                    for (lo, hi) in pieces:
                        n = hi - lo
                        u = nxt("up3", 3)
                        pg, pu = pbank[2 * u], pbank[2 * u + 1]
                        hkeys = [("hb", kc, b) for kc in range(8) for b in blocks_of(lo, hi)]
                        mm_group(pu[:, :n], [(wu[:, kc, 128:256], hb[:, kc, lo:hi]) for kc in range(8)],
                                 hkeys + [("ring", r)], [("pb", 2 * u + 1)])
                        mm_group(pg[:, :n], [(wu[:, kc, 0:128], hb[:, kc, lo:hi]) for kc in range(8)],
                                 hkeys + [("ring", r)], [("pb", 2 * u)])
                        s = nxt("sg", 2)
                        P.op("act", lambda e, pg=pg, n=n, s=s: e.activation(out=sg[:, s, :n], in_=pg[:, :n], func=AF.Silu),
                             [("pb", 2 * u)], [("sg", s)])
                        P.op("dve", lambda e, pu=pu, n=n, s=s, jj=jj, lo=lo, hi=hi: e.tensor_tensor(
                            out=b16[:, jj, lo:hi], in0=sg[:, s, :n], in1=pu[:, :n], op=ALU.mult),
                            [("sg", s), ("pb", 2 * u + 1)], [("b16", jj, b) for b in blocks_of(lo, hi)])
                        if j == 0 and hook is not None:
                            hook(lo)
                    bg_step(6)
                P.phase = "t%d.ffn%d.down%d" % (ti, f + 1, gi)
                nj = j1 - j0
                slots = []
                jq = j0
                while jq < j1:
                    if jq % 2 == 0 and jq + 1 < j1:
                        r = load_unit(sc_dn[f][jq:jq + 2].rearrange("j p d -> p j d"), ("sc_dn", f, jq),
                                      view=lambda a: a.rearrange("p (j d) -> p j d", j=2))
                        slots.append((r, 0))
                        slots.append((r, 1))
                        jq += 2
                    else:
                        r = load_unit(sc_dn[f][jq], ("sc_dn", f, jq),
                                      view=lambda a: a[:, 0:D])
                        slots.append((r, 0))
                        jq += 1
                if last and before_last_down is not None:
                    before_last_down()
                for c in range(nchunks):
                    rows = 128 if c < 8 else 32
                    lo = c * 128
                    slot = xslot(ti, c)
                    for half in range(2):
                        a = (4, 5, 0, 1, 2, 3)[nxt("dacc", 6)]
                        pa = pbank[a]
                        pairs = []
                        for jj in range(nj):
                            r, o = slots[jj]
                            pairs.append((b16[:, jj, lo:lo + rows],
                                          ring[:, r, o * D + half * 512:o * D + (half + 1) * 512]))
                        mm_group(pa[:rows, :], pairs,
                                 [("b16", jj, c) for jj in range(nj)] + [("ring", r) for r, _ in slots],
                                 [("pb", a)])
                        P.op("dve", lambda e, pa=pa, rows=rows, slot=slot, half=half: e.scalar_tensor_tensor(
                            out=xs[:rows, slot, half * 512:(half + 1) * 512], in0=pa[:rows, :], scalar=0.5,
                            in1=xs[:rows, slot, half * 512:(half + 1) * 512], op0=ALU.mult, op1=ALU.add),
                            [("pb", a), ("xs", slot)], [("xs", slot)])
                    if last and on_chunk is not None:
                        on_chunk(c)

        pending_T = []

        def norm_pre(ti, c):
            rows = 128 if c < 8 else 32
            slot = xslot(ti, c)
            col = rms_stats(slot, rows)
            h = norm_scale(slot, rows, col)
            pending_T.append((h, c))

        def flush_T(keep=0):
            while len(pending_T) > keep:
                h, c = pending_T.pop(0)
                norm_transpose(h, c)

        def norm_all(ti, nchunks):
            for c in range(nchunks):
                norm_pre(ti, c)
                flush_T(keep=2)
            flush_T(0)

        def tcols(lo, hi):
            if lo < 1024:
                return [(0, hi - lo, 16 + lo)]
            return [(0, 16, 0), (16, 32, 1040)]

        def mixer(ti, hook=None):
            dma("pool", pcorr[:, ti % 2, :], pcorr_d[:, ti * 64:(ti + 1) * 64].partition_broadcast(128), [],
                [("pcorr", ti % 2)])
            P.phase = "t%d.win_vg" % ti
            for i in range(4):
                r = load_unit(sc_in[2 + i], ("sc_in", 2 + i))
                wu = ring[:, r, :].rearrange("p (k c) -> p k c", k=8)
                for (lo, hi) in PIECES_A:
                    n = hi - lo
                    u = nxt("up", 2)
                    pv, pgt = pbank[2 * u], pbank[2 * u + 1]
                    hkeys = [("hb", kc, b) for kc in range(8) for b in blocks_of(lo, hi)]
                    mm_group(pv[:, :n], [(wu[:, kc, 0:128], hb[:, kc, lo:hi]) for kc in range(8)],
                             hkeys + [("ring", r)], [("pb", 2 * u)])
                    mm_group(pgt[:, :n], [(wu[:, kc, 128:256], hb[:, kc, lo:hi]) for kc in range(8)],
                             hkeys + [("ring", r)], [("pb", 2 * u + 1)])
                    s = nxt("sg", 2)
                    P.op("act", lambda e, pgt=pgt, n=n, s=s: e.activation(out=sg[:, s, :n], in_=pgt[:, :n],
                                                                          func=AF.Tanh, scale=0.5),
                         [("pb", 2 * u + 1)], [("sg", s)])
                    for (p0, p1, t0) in tcols(lo, hi):
                        P.op("dve", lambda e, pv=pv, s=s, p0=p0, p1=p1, t0=t0, i=i: e.scalar_tensor_tensor(
                            out=b16[:, i, t0:t0 + (p1 - p0)], in0=sg[:, s, p0:p1], scalar=1.0, in1=pv[:, p0:p1],
                            op0=ALU.add, op1=ALU.mult),
                            [("sg", s), ("pb", 2 * u)], [("b16", i, b) for b in blocks_of(t0, t0 + p1 - p0)])
                    if i == 0 and hook is not None:
                        hook(lo)

            P.phase = "t%d.win_pool" % ti
            for u in range(2):
                r = load_unit(sc_in[u], ("sc_in", u))
                wu = ring[:, r, :].rearrange("p (k c) -> p k c", k=8)
                for gg in range(2):
                    g = 2 * u + gg
                    ur = g % 2
                    for (lo, hi) in PIECES_A:
                        n = hi - lo
                        a = 4 + nxt("acc", 2)
                        pa = pbank[a]
                        hkeys = [("hb", kc, b) for kc in range(8) for b in blocks_of(lo, hi)]
                        mm_group(pa[:, :n], [(wu[:, kc, gg * 128:(gg + 1) * 128], hb[:, kc, lo:hi]) for kc in range(8)],
                                 hkeys + [("ring", r)], [("pb", a)])
                        for (p0, p1, t0) in tcols(lo, hi):
                            P.op("act", lambda e, pa=pa, p0=p0, p1=p1, t0=t0, ur=ur: e.activation(
                                out=upool[:, ur, t0:t0 + (p1 - p0)], in_=pa[:, p0:p1], func=AF.Copy),
                                [("pb", a)], [("upool", ur, b) for b in blocks_of(t0, t0 + p1 - p0)])
                    U = upool[:, ur, :]
                    ukeys = [("upool", ur, b) for b in range(9)]
                    w = 2 ** (g + 1)
                    hlf = w // 2
                    cur = U
                    curkeys = ukeys
                    ext = 1056
                    step = 1
                    for lvl in range(g + 1):
                        ext2 = ext - step
                        dst = ptmp[:, lvl % 2, :]
                        P.op("dve", lambda e, cur=cur, dst=dst, ext2=ext2, step=step: e.tensor_tensor(
                            out=dst[:, 0:ext2], in0=cur[:, 0:ext2], in1=cur[:, step:step + ext2], op=ALU.add),
                            curkeys, [("ptmp", lvl % 2)])
                        cur = dst
                        curkeys = [("ptmp", lvl % 2)]
                        ext = ext2
                        step *= 2
                    sidx = g % 2
                    other = ptmp[:, (g + 1) % 2, :]
                    P.op("dve", lambda e, cur=cur, other=other, hlf=hlf, w=w: e.tensor_scalar(
                        out=other[:, 0:TT], in0=cur[:, 16 - hlf:16 - hlf + TT], scalar1=1.0 / w, scalar2=None,
                        op0=ALU.mult), curkeys, [("ptmp", (g + 1) % 2)])
                    okeys = [("ptmp", (g + 1) % 2)]
                    cb = g * 16
                    pq = ti % 2
                    P.op("dve", lambda e, other=other, cb=cb, pq=pq: e.tensor_tensor(
                        out=other[:, 0:8], in0=other[:, 0:8], in1=pcorr[:, pq, cb:cb + 8], op=ALU.mult),
                        okeys + [("pcorr", pq)], okeys)
                    P.op("dve", lambda e, other=other, cb=cb, pq=pq: e.tensor_tensor(
                        out=other[:, TT - 8:TT], in0=other[:, TT - 8:TT], in1=pcorr[:, pq, cb + 8:cb + 16], op=ALU.mult),
                        okeys + [("pcorr", pq)], okeys)
                    prow = 4 + g
                    P.op("dve", lambda e, other=other, U=U, prow=prow: e.tensor_tensor(
                        out=b16[:, prow, 0:TT], in0=other[:, 0:TT], in1=U[:, 16:16 + TT], op=ALU.subtract),
                        okeys + ukeys, [("b16", prow, b) for b in range(8)])
        pend_pool = []

        def mixer_tail(ti):
            def conv_piece(pi, after_chunk=None):
                lo, hi = PIECES_B[pi]
                pm, pe2 = pbank[2 * pi], pbank[2 * pi + 1]

                def stats(i):
                    q = i % 2
                    P.op("pe", lambda e: e.matmul(pm[:, :], onesb[:, :], ysq[:, q, 0, :], start=(i == 0), stop=(i == 3)),
                         [("ysq", q, 0), ("onesb",)], [("pb", 2 * pi)])
                    P.op("pe", lambda e: e.matmul(pe2[:, :], onesb[:, :], ysq[:, q, 1, :], start=(i == 0), stop=(i == 3)),
                         [("ysq", q, 1), ("onesb",)], [("pb", 2 * pi + 1)])

                for i in range(4):
                    a = 4 + nxt("acc", 2)
                    pa = pbank[a]
                    pairs = [(diag[:, i, k, :], b16[:, i, lo + k + 1:lo + k + 1 + 512]) for k in range(31)]
                    mm_group(pa[:, :], pairs,
                             [("b16", i, b) for b in blocks_of(lo + 1, lo + 31 + 512)] +
                             [("diag", i, k) for k in range(31)], [("pb", a)])
                    P.op("act", lambda e, pa=pa, i=i: e.activation(
                        out=ybuf[:, pi, i, :], in_=pa[:, :], func=AF.Identity, bias=small[:, 32 + i:33 + i]),
                        [("pb", a), ("small",)], [("ybuf", pi, i)])
                    q = i % 2
                    P.op("act", lambda e, pa=pa, i=i, q=q: e.activation(out=ysq[:, q, 0, :], in_=pa[:, :], func=AF.Identity,
                                                                        bias=small[:, 32 + i:33 + i]),
                         [("pb", a), ("small",)], [("ysq", q, 0)])
                    P.op("act", lambda e, pa=pa, i=i, q=q: e.activation(out=ysq[:, q, 1, :], in_=pa[:, :], func=AF.Square,
                                                                        bias=small[:, 32 + i:33 + i]),
                         [("pb", a), ("small",)], [("ysq", q, 1)])
                    if i >= 1:
                        stats(i - 1)
                    if after_chunk is not None:
                        after_chunk(i)
                stats(3)

            def ln_head(pi, part):
                pm, pe2 = pbank[2 * pi], pbank[2 * pi + 1]
                sl = slice(pi * 512, (pi + 1) * 512)
                if part >= 1:
                    if part == 1:
                        P.op("act", lambda e: e.activation(out=lnt[:, 1, sl], in_=lnt[:, 1, sl], func=AF.Sqrt),
                             [("lnt", 1, pi)], [("lnt", 1, pi)])
                    hs = slice(pi * 512 + (part - 1) * 256, pi * 512 + part * 256)
                    P.op("dve", lambda e: e.reciprocal(out=lnt[:, 1, hs], in_=lnt[:, 1, hs]),
                         [("lnt", 1, pi)], [("lnt", 1, pi)])
                    return
                P.op("act", lambda e: e.activation(out=lnt[:, 0, sl], in_=pm[:, :], func=AF.Copy),
                     [("pb", 2 * pi)], [("lnt", 0, pi)])
                P.op("act", lambda e: e.activation(out=lnt[:, 1, sl], in_=pm[:, :], func=AF.Square),
                     [("pb", 2 * pi)], [("lnt", 1, pi)])
                P.op("dve", lambda e: e.scalar_tensor_tensor(out=lnt[:, 1, sl], in0=pe2[:, :], scalar=EPS, in1=lnt[:, 1, sl],
                                                             op0=ALU.add, op1=ALU.subtract),
                     [("pb", 2 * pi + 1), ("lnt", 1, pi)], [("lnt", 1, pi)])
                P.op("dve", lambda e: e.tensor_scalar(out=lnt[:, 1, sl], in0=lnt[:, 1, sl], scalar1=1e-12,
                                                      scalar2=None, op0=ALU.max),
                     [("lnt", 1, pi)], [("lnt", 1, pi)])

            def ln_chunk(pi, i):
                lo, hi = PIECES_B[pi]
                sl = slice(pi * 512, (pi + 1) * 512)
                P.op("pool", lambda e: e.tensor_tensor(out=ybuf[:, pi, i, :], in0=ybuf[:, pi, i, :], in1=lnt[:, 0, sl],
                                                       op=ALU.subtract), [("ybuf", pi, i), ("lnt", 0, pi)], [("ybuf", pi, i)])
                P.op("pool", lambda e: e.tensor_tensor(out=ybuf[:, pi, i, :], in0=ybuf[:, pi, i, :], in1=lnt[:, 1, sl],
                                                       op=ALU.mult), [("ybuf", pi, i), ("lnt", 1, pi)], [("ybuf", pi, i)])
                P.op("act", lambda e: e.activation(out=hb[:, 4 + i, lo:hi], in_=ybuf[:, pi, i, :], func=AF.Silu,
                                                   scale=small[:, 36 + i:37 + i], bias=small[:, 40 + i:41 + i]),
                     [("ybuf", pi, i), ("small",)], [("hb", 4 + i, b) for b in blocks_of(lo, hi)])

            P.phase = "t%d.convA" % ti
            conv_piece(0)
            P.phase = "t%d.poolw" % ti
            for g in range(4):
                for (lo, hi) in PIECES_B:
                    a = 4 + nxt("acc", 2)
                    pa = pbank[a]
                    prow = 4 + g
                    mm_group(pa[:, :], [(poolw[:, g, :], b16[:, prow, lo:hi])],
                             [("b16", prow, b) for b in blocks_of(lo, hi)] + [("poolw",)], [("pb", a)])
                    P.op("act", lambda e, pa=pa, g=g, lo=lo, hi=hi: e.activation(
                        out=hb[:, g, lo:hi], in_=pa[:, :], func=AF.Identity,
                        scale=small[:, 28 + g:29 + g], bias=pbias[:, g:g + 1]),
                        [("pb", a), ("small",), ("pbias",)], [("hb", g, b) for b in blocks_of(lo, hi)])
            P.phase = "t%d.convB" % ti

            def lnA(i):
                if i == 0:
                    ln_head(0, 0)
                elif i == 1:
                    ln_head(0, 1)
                    ln_head(0, 2)
                elif i == 2:
                    ln_chunk(0, 0)
                    ln_chunk(0, 1)
                elif i == 3:
                    ln_chunk(0, 2)
                    ln_chunk(0, 3)
            conv_piece(1, after_chunk=lnA)
            P.phase = "t%d.wout" % ti
            slots = []
            for q in range(4):
                r = load_unit(sc_out[2 * q:2 * q + 2].rearrange("j p d -> p j d"), ("sc_out", q),
                              view=lambda a: a.rearrange("p (j d) -> p j d", j=2))
                slots += [(r, 0), (r, 1)]

            def wout_part(c, cis):
                lo = c * 128
                slot = xslot(ti, c)
                for half in range(2):
                    a = 4 + nxt("acc", 2)
                    pa = pbank[a]
                    pairs = [(hb[:, ci, lo:lo + 128],
                              ring[:, slots[ci][0], slots[ci][1] * D + half * 512:slots[ci][1] * D + (half + 1) * 512])
                             for ci in cis]
                    mm_group(pa[:, :], pairs, [("hb", ci, c) for ci in cis] + [("ring", slots[ci][0]) for ci in cis],
                             [("pb", a)])
                    P.op("dve", lambda e, pa=pa, slot=slot, half=half: e.tensor_tensor(
                        out=xs[:, slot, half * 512:(half + 1) * 512], in0=pa[:, :],
                        in1=xs[:, slot, half * 512:(half + 1) * 512], op=ALU.add),
                        [("pb", a), ("xs", slot)], [("xs", slot)])

            for c in range(8):
                wout_part(c, range(0, 4))
                if c == 0:
                    ln_head(1, 0)
                elif c == 2:
                    ln_head(1, 1)
                elif c == 3:
                    ln_head(1, 2)
                elif c >= 4:
                    ln_chunk(1, c - 4)
            P.phase = "t%d.wout2" % ti
            for c in range(8):
                wout_part(c, range(4, 8))
                norm_pre(ti, c)
                flush_T(keep=2)

        def load_chunk(ti, c):
            slot = free_slots.pop(0)
            slotmap[(ti, c)] = slot
            if c < 8:
                dma("pool", xs[:, slot, :], xt[ti, c * 128:(c + 1) * 128, :], [], [("xs", slot)])
            else:
                dma("pool", xs[:32, slot, :], xt[ti, TT:TT + 32, :], [], [("xs", slot)])

        nxt_state = {"to_load": [], "to_norm": []}

        def load_more(ti):
            while free_slots and nxt_state["to_load"]:
                c = nxt_state["to_load"].pop(0)
                load_chunk(ti, c)
                nxt_state["to_norm"].append(c)

        def norm_more(ti):
            while nxt_state["to_norm"] and len(pending_T) < 3:
                norm_pre(ti, nxt_state["to_norm"].pop(0))

        fin_pending = []

        def final_stats(ti, c):
            slot = xslot(ti, c)
            col = rms_stats(slot, 128)
            fin_pending.append((ti, c, slot, col))

        def final_finish():
            ti, c, slot, col = fin_pending.pop(0)
            P.op("dve", lambda e: e.scalar_tensor_tensor(
                out=xs[:, slot, :], in0=xs[:, slot, :], scalar=rst[:, col:col + 1], in1=gfb[:, :],
                op0=ALU.mult, op1=ALU.mult),
                [("xs", slot), ("rst", col), ("gfb",)], [("xs", slot)])
            dma("sp", yt[ti, c * 128:(c + 1) * 128, :], xs[:, slot, :], [("xs", slot)], [("yt", ti, c)])
            free_slot(ti, c)

        def flush_all(lo):
            if lo == 0:
                flush_T(1 if (pending_T and pending_T[-1][1] == 8) else 0)
            else:
                flush_T(0)

        jit_advance(0)
        for c in range(NCH):
            load_chunk(0, c)
        norm_all(0, NCH)
        for ti in range(ntiles):
            has_next = ti + 1 < ntiles

            def mix_cb(c, ti=ti):
                norm_pre(ti, c)
                if c == 8:
                    free_slot(ti, 8)
                flush_T(keep=2)

            ffn(ti, 0, NCH, PIECES_A, on_chunk=mix_cb, hook=flush_all)
            mixer(ti, hook=flush_all)
            flush_T(0)
            mixer_tail(ti)

            def pre_last(ti=ti, has_next=has_next):
                if has_next:
                    nxt_state["to_load"] = list(range(NCH))
                    nxt_state["to_norm"] = []
                    load_more(ti + 1)
                    norm_more(ti + 1)
                    jit["on"] = False
                    for j in range(4):
                        preload_unit(sc_gu[0][j], ("sc_gu", 0, j))

            def fin_cb(c, ti=ti, has_next=has_next):
                final_stats(ti, c)
                if c >= 1:
                    final_finish()
                if has_next:
                    norm_more(ti + 1)
                    flush_T(keep=2)
                    load_more(ti + 1)

            ffn(ti, 1, 8, PIECES_B, before_last_down=pre_last, on_chunk=fin_cb, hook=flush_all)
            if has_next:
                load_more(ti + 1)
                while nxt_state["to_norm"] or len(pending_T) > 2:
                    norm_more(ti + 1)
                    flush_T(keep=2)
                flush_T(1 if (pending_T and pending_T[-1][1] == 8) else 0)
            final_finish()
        flush_T(0)

        with nc.Block() as block:
            fw = [((q, i), P.dma_count[q][i] * 16) for q in NDSEM for i in range(NDSEM[q]) if P.dma_count[q][i] > 0]
            P.emit(nc, block, esem, dsem, fw)
    return nc


def _core_inputs(x_prompt, x_sample, core, ntiles=12):
    segs = []
    for s in range(2):
        segs.append((x_sample[2 * core + s], 0, 4096))
    b, q = core // 4, core % 4
    segs.append((x_prompt[b], q * 4096, (q + 1) * 4096))
    xt = np.zeros((ntiles, NCOL, D), np.float32)
    pc = np.ones((ntiles, 4, 16), np.float32)
    ti = 0
    for (seq, s0, s1) in segs:
        L = seq.shape[0]
        for i in range((s1 - s0) // TT):
            if ti >= ntiles:
                break
            t0 = s0 + i * TT
            xt[ti, :TT] = seq[t0:t0 + TT]
            if t0 - HALO >= 0:
                xt[ti, TT:TT + HALO] = seq[t0 - HALO:t0]
            if t0 + TT + HALO <= L:
                xt[ti, TT + HALO:] = seq[t0 + TT:t0 + TT + HALO]
            for g in range(4):
                w = 2 ** (g + 1)
                h = w // 2
                for m in range(8):
                    t = t0 + m
                    c = min(t + h, L) - max(t - h, 0)
                    pc[ti, g, m] = w / c
                    t = t0 + TT - 8 + m
                    c = min(t + h, L) - max(t - h, 0)
                    pc[ti, g, 8 + m] = w / c
            ti += 1
    return xt, pc.reshape(1, -1)


def _small(inp):
    sm = np.zeros((128, NSMALL), np.float32)
    sm[:, 0:8] = inp["ffn1_norm"][0].reshape(8, 128).T
    sm[:, 8:16] = inp["mix_norm"][0].reshape(8, 128).T
    sm[:, 16:24] = inp["ffn2_norm"][0].reshape(8, 128).T
    sm[:, 24:28] = inp["pool_b"][0].T
    sm[:, 28:32] = inp["pool_scale"][0].reshape(4, 128).T
    sm[:, 32:36] = inp["dw_b"][0].reshape(4, 128).T
    sm[:, 36:40] = inp["conv_ln_g"][0].reshape(4, 128).T
    sm[:, 40:44] = inp["conv_ln_b"][0].reshape(4, 128).T
    dw = inp["dw_w"][0]
    sm[:, 44:168] = dw.reshape(31, 4, 128).transpose(2, 1, 0).reshape(128, 124)
    return sm


_NC_CACHE = {}


def _shared_maps(inp):
    f = lambda a: np.ascontiguousarray(np.asarray(a, dtype=np.float32))
    return {
        "w_gu1": f(inp["ffn1_w_gu"][0]), "w_gu2": f(inp["ffn2_w_gu"][0]),
        "w_dn1": f(inp["ffn1_w_down"][0]), "w_dn2": f(inp["ffn2_w_down"][0]),
        "w_in": f(inp["w_in"][0]), "w_out": f(inp["w_out"][0]),
        "pool_w": f(inp["pool_w"][0].reshape(512, 128)),
        "small": _small(inp), "gfin": f(inp["final_norm"].reshape(1, D)),
        "ident": np.eye(128, dtype=np.float32),
    }


def kernel(**inputs):
    inp = {k: np.asarray(v) for k, v in inputs.items()}
    xp, xsm = inp["x_prompt"], inp["x_sample"]
    ntiles = 12
    if ntiles not in _NC_CACHE:
        _NC_CACHE[ntiles] = build_nc(ntiles)
    nc = _NC_CACHE[ntiles]
    shared = _shared_maps(inp)
    in_maps = []
    for core in range(8):
        xt, pc = _core_inputs(xp, xsm, core)
        m = dict(shared)
        m["xt"] = xt
        m["pcorr"] = pc
        in_maps.append(m)
    res = run_bass_kernel_spmd(nc, in_maps, core_ids=list(range(8)))
    y_prompt = np.empty_like(xp, dtype=np.float32)
    y_sample = np.empty_like(xsm, dtype=np.float32)
    for core in range(8):
        yt = np.asarray(res.results[core]["yt"]).reshape(12, TT, D)
        y_sample[2 * core] = yt[0:4].reshape(4096, D)
        y_sample[2 * core + 1] = yt[4:8].reshape(4096, D)
        b, q = core // 4, core % 4
        y_prompt[b, q * 4096:(q + 1) * 4096] = yt[8:12].reshape(4096, D)
    return (y_prompt, y_sample)
```
